# Optimizing a Trainium2 kernel written in Bass

```python
import math
import jax, jax.numpy as jnp
from jax import lax
import numpy as np

D_MODEL = 1024
BATCH = 8
SEQ = 8192
DEPTH = 1
DEC_BATCH = 32
DEC_SEQ = 2048
PAST_LEN = 128

D_MIX = D_MODEL
D_FOURIER = D_MIX // 4
N_FOURIER_GROUPS = 4
FOURIER_GROUP_DIM = D_FOURIER // N_FOURIER_GROUPS
D_ATTN = D_MIX - D_FOURIER
HEAD_DIM = 64
N_HEADS = D_ATTN // HEAD_DIM
D_PROJ = D_FOURIER + 3 * D_ATTN
D_FF = 2816
WINDOW_DILATIONS = ((128, 1), (512, 4), (2048, 16))
RADIUS = 64
ROPE_THETA = 10000.0
RMS_EPS = 1e-6
MASK_VALUE = -1e30

kernel_name = "hybrid_fnet_dilated_attn_macaron_encoder"


def rms_norm(x, g):
    xf = x.astype(jnp.float32)
    inv = lax.rsqrt(jnp.mean(xf * xf, axis=-1, keepdims=True) + RMS_EPS)
    return (xf * inv * g.astype(jnp.float32)).astype(x.dtype)


def swiglu(h, w_gate, w_up, w_down):
    return (jax.nn.silu(h @ w_gate) * (h @ w_up)) @ w_down


def apply_rope(t):
    S = t.shape[1]
    inv_freq = ROPE_THETA ** (-jnp.arange(0, HEAD_DIM, 2, dtype=jnp.float32) / HEAD_DIM)
    freqs = jnp.arange(S, dtype=jnp.float32)[:, None] * inv_freq[None, :]
    emb = jnp.concatenate([freqs, freqs], axis=-1)[:, None, :]
    cos, sin = jnp.cos(emb), jnp.sin(emb)
    tf = t.astype(jnp.float32)
    t1, t2 = tf[..., : HEAD_DIM // 2], tf[..., HEAD_DIM // 2:]
    rot = jnp.concatenate([-t2, t1], axis=-1)
    return (tf * cos + rot * sin).astype(t.dtype)


def fourier_mix(u, w_f):
    B, S, _ = u.shape
    ug = u.reshape(B, S, N_FOURIER_GROUPS, FOURIER_GROUP_DIM).astype(jnp.float32)
    f = jnp.fft.fft2(ug, axes=(1, 3), norm="ortho").real.astype(u.dtype)
    return jnp.einsum('bsgc,gce->bsge', f, w_f).reshape(B, S, D_FOURIER)


def dilated_window_attention(q, k, v, dilation):
    B, S, H, Dh = q.shape
    d = dilation
    Lm = S // d
    nb = -(-Lm // RADIUS)
    Lp = nb * RADIUS
    pad = Lp - Lm
    qs = jnp.pad(q.reshape(B, Lm, d, H, Dh), ((0, 0), (0, pad), (0, 0), (0, 0), (0, 0)))
    kv_pad = ((0, 0), (RADIUS, pad + RADIUS), (0, 0), (0, 0), (0, 0))
    ks = jnp.pad(k.reshape(B, Lm, d, H, Dh), kv_pad)
    vs = jnp.pad(v.reshape(B, Lm, d, H, Dh), kv_pad)
    qb = qs.reshape(B, nb, RADIUS, d, H, Dh)
    kb = ks.reshape(B, nb + 2, RADIUS, d, H, Dh)
    vb = vs.reshape(B, nb + 2, RADIUS, d, H, Dh)
    kwin = jnp.concatenate([kb[:, :-2], kb[:, 1:-1], kb[:, 2:]], axis=2)
    vwin = jnp.concatenate([vb[:, :-2], vb[:, 1:-1], vb[:, 2:]], axis=2)
    logits = jnp.einsum('bnqrhd,bnkrhd->bnrhqk', qb, kwin).astype(jnp.float32) * (HEAD_DIM ** -0.5)
    qi = jnp.arange(RADIUS)[:, None]
    ki = jnp.arange(3 * RADIUS)[None, :]
    in_window = jnp.abs(ki - RADIUS - qi) <= RADIUS
    key_pos = jnp.arange(nb)[:, None] * RADIUS - RADIUS + jnp.arange(3 * RADIUS)[None, :]
    in_range = (key_pos >= 0) & (key_pos < Lm)
    valid = in_window[None, :, :] & in_range[:, None, :]
    logits = jnp.where(valid[None, :, None, None, :, :], logits, MASK_VALUE)
    lse = jax.nn.logsumexp(logits, axis=-1)
    p = jnp.exp(logits - lse[..., None]).astype(v.dtype)
    out = jnp.einsum('bnrhqk,bnkrhd->bnqrhd', p, vwin)
    out = out.reshape(B, Lp, d, H, Dh)[:, :Lm].reshape(B, S, H, Dh)
    lse = jnp.transpose(lse, (0, 1, 4, 2, 3)).reshape(B, Lp, d, H)[:, :Lm].reshape(B, S, H)
    return out, lse


def dilated_mixture_attention(q, k, v):
    outs, lses = [], []
    for window, dilation in WINDOW_DILATIONS:
        o, l = dilated_window_attention(q, k, v, dilation)
        outs.append(o.astype(jnp.float32))
        lses.append(l)
    w = jax.nn.softmax(jnp.stack(lses, axis=0), axis=0)
    out = jnp.sum(w[..., None] * jnp.stack(outs, axis=0), axis=0)
    B, S = q.shape[:2]
    return out.astype(q.dtype).reshape(B, S, D_ATTN)


def encoder_layer(x, ffn1_norm, ffn1_w_gate, ffn1_w_up, ffn1_w_down,
                  mix_norm, w_in, fourier_w, fourier_out_norm, attn_out_norm, w_out,
                  ffn2_norm, ffn2_w_gate, ffn2_w_up, ffn2_w_down):
    B, S, _ = x.shape
    x = x + 0.5 * swiglu(rms_norm(x, ffn1_norm), ffn1_w_gate, ffn1_w_up, ffn1_w_down)
    h = rms_norm(x, mix_norm)
    proj = h @ w_in
    u_f = proj[..., :D_FOURIER]
    q = proj[..., D_FOURIER:D_FOURIER + D_ATTN].reshape(B, S, N_HEADS, HEAD_DIM)
    k = proj[..., D_FOURIER + D_ATTN:D_FOURIER + 2 * D_ATTN].reshape(B, S, N_HEADS, HEAD_DIM)
    v = proj[..., D_FOURIER + 2 * D_ATTN:].reshape(B, S, N_HEADS, HEAD_DIM)
    q, k = apply_rope(q), apply_rope(k)
    o_f = fourier_mix(u_f, fourier_w)
    o_a = dilated_mixture_attention(q, k, v)
    mix = jnp.concatenate([rms_norm(o_f, fourier_out_norm), rms_norm(o_a, attn_out_norm)], axis=-1)
    x = x + mix @ w_out
    x = x + 0.5 * swiglu(rms_norm(x, ffn2_norm), ffn2_w_gate, ffn2_w_up, ffn2_w_down)
    return x


def setup_inputs(seed: int = 0) -> dict:
    key = jax.random.key(seed)
    ks = jax.random.split(key, 20)
    f32 = jnp.float32

    def w(k, shape, fan_in):
        return jax.random.normal(k, shape, f32) * (fan_in ** -0.5)

    def gain(k, shape):
        return 1.0 + 0.02 * jax.random.normal(k, shape, f32)

    return {
        "x_prompt": jax.random.normal(ks[0], (BATCH, SEQ, D_MODEL), f32),
        "x_sample": jax.random.normal(ks[1], (DEC_BATCH, DEC_SEQ, D_MODEL), f32),
        "ffn1_norm": gain(ks[2], (DEPTH, D_MODEL)),
        "ffn1_w_gate": w(ks[3], (DEPTH, D_MODEL, D_FF), D_MODEL),
        "ffn1_w_up": w(ks[4], (DEPTH, D_MODEL, D_FF), D_MODEL),
        "ffn1_w_down": w(ks[5], (DEPTH, D_FF, D_MODEL), D_FF),
        "mix_norm": gain(ks[6], (DEPTH, D_MODEL)),
        "w_in": w(ks[7], (DEPTH, D_MODEL, D_PROJ), D_MODEL),
        "fourier_w": w(ks[8], (DEPTH, N_FOURIER_GROUPS, FOURIER_GROUP_DIM, FOURIER_GROUP_DIM), FOURIER_GROUP_DIM),
        "fourier_out_norm": gain(ks[9], (DEPTH, D_FOURIER)),
        "attn_out_norm": gain(ks[10], (DEPTH, D_ATTN)),
        "w_out": w(ks[11], (DEPTH, D_MIX, D_MODEL), D_MIX),
        "ffn2_norm": gain(ks[12], (DEPTH, D_MODEL)),
        "ffn2_w_gate": w(ks[13], (DEPTH, D_MODEL, D_FF), D_MODEL),
        "ffn2_w_up": w(ks[14], (DEPTH, D_MODEL, D_FF), D_MODEL),
        "ffn2_w_down": w(ks[15], (DEPTH, D_FF, D_MODEL), D_FF),
        "final_norm": gain(ks[16], (D_MODEL,)),
    }


def trunk(x, ffn1_norm, ffn1_w_gate, ffn1_w_up, ffn1_w_down, mix_norm, w_in, fourier_w,
          fourier_out_norm, attn_out_norm, w_out, ffn2_norm, ffn2_w_gate, ffn2_w_up, ffn2_w_down,
          final_norm):
    for l in range(DEPTH):
        x = encoder_layer(x, ffn1_norm[l], ffn1_w_gate[l], ffn1_w_up[l], ffn1_w_down[l],
                          mix_norm[l], w_in[l], fourier_w[l], fourier_out_norm[l], attn_out_norm[l],
                          w_out[l], ffn2_norm[l], ffn2_w_gate[l], ffn2_w_up[l], ffn2_w_down[l])
    return rms_norm(x, final_norm)


def reference(x_prompt, x_sample, ffn1_norm, ffn1_w_gate, ffn1_w_up, ffn1_w_down, mix_norm, w_in,
              fourier_w, fourier_out_norm, attn_out_norm, w_out, ffn2_norm, ffn2_w_gate, ffn2_w_up,
              ffn2_w_down, final_norm):
    y_prompt = trunk(x_prompt, ffn1_norm, ffn1_w_gate, ffn1_w_up, ffn1_w_down, mix_norm, w_in,
                     fourier_w, fourier_out_norm, attn_out_norm, w_out, ffn2_norm, ffn2_w_gate,
                     ffn2_w_up, ffn2_w_down, final_norm)
    y_sample = trunk(x_sample, ffn1_norm, ffn1_w_gate, ffn1_w_up, ffn1_w_down, mix_norm, w_in,
                     fourier_w, fourier_out_norm, attn_out_norm, w_out, ffn2_norm, ffn2_w_gate,
                     ffn2_w_up, ffn2_w_down, final_norm)
    return (y_prompt, y_sample)
```

```python
import contextlib
import math

import numpy as np
import ml_dtypes

import concourse.bass as bass
import concourse.mybir as mybir
from concourse.bass_utils import run_bass_kernel_spmd

F32 = mybir.dt.float32
BF16 = mybir.dt.bfloat16
AF = mybir.ActivationFunctionType
ALU = mybir.AluOpType

D = 1024
DFF = 2816
NF = DFF // 128
T = 512
EPS = 1e-6
NCORES = 8


class Buf:
    __slots__ = ("name", "w", "r", "dsem", "dcnt", "dram")

    def __init__(self, name, dram=False):
        self.name = name
        self.w = {}
        self.r = {}
        self.dsem = None
        self.dcnt = 0
        self.dram = dram


class Sched:
    def __init__(self, nc, es):
        self.nc = nc
        self.es = es
        self.sems = {}
        self.eng = {}
        for name, h in (("pe", nc.tensor), ("act", nc.scalar), ("dve", nc.vector),
                        ("pool", nc.gpsimd), ("sp", nc.sync)):
            sn = "s_" + name
            self.sems[sn] = es.enter_context(nc.semaphore(sn))
            self.eng[name] = dict(h=h, sn=sn, cnt=0, seen={}, name=name)
        self.pe_pr = []
        self.pe_pw = []
        self.nbuf = 0
        self.dtotal = {}
        self.allbufs = []

    def buf(self, name, dram=False):
        self.nbuf += 1
        b = Buf(f"{name}_{self.nbuf}", dram)
        if not dram:
            self.allbufs.append(b)
        return b

    def barrier(self):
        for en in self.eng:
            self.wait_all(en, self.allbufs)
        self.allbufs = [b for b in self.allbufs if b.name.startswith("ps") or b.name.startswith("consts")]

    def _deps(self, e, reads, writes):
        deps = {}
        own = e["sn"]
        ispe = e["name"] == "pe"

        def add(tok, war):
            for sn, v in tok.items():
                if sn == own and (ispe or war):
                    continue
                if deps.get(sn, 0) < v:
                    deps[sn] = v
        for b in reads:
            add(b.w, False)
        for b in writes:
            if b.dram:
                continue
            add(b.w, False)
            add(b.r, True)
        self._emit_waits(e, deps)

    def _emit_waits(self, e, deps):
        for sn, v in deps.items():
            if sn in self.dtotal:
                v = max(v, self.dtotal[sn])
            if e["seen"].get(sn, 0) >= v:
                continue
            e["h"].wait_ge(self.sems[sn], v)
            e["seen"][sn] = v

    def op(self, en, fn, reads=(), writes=(), sig=True):
        e = self.eng[en]
        if en != "pe":
            for b in writes:
                assert all(b is not p for p in self.pe_pr), f"write to {b.name} while PE read pending"
                assert all(b is not p for p in self.pe_pw), f"write to {b.name} while PE write pending"
            for b in reads:
                assert all(b is not p for p in self.pe_pw), f"read of {b.name} while PE write pending"
        self._deps(e, reads, writes)
        inst = fn(e["h"])
        if en == "pe" and not sig:
            self.pe_pr.extend(reads)
            self.pe_pw.extend(writes)
            return inst
        e["cnt"] += 1
        inst.then_inc(self.sems[e["sn"]], 1)
        sn, v = e["sn"], e["cnt"]
        rl, wl = list(reads), list(writes)
        if en == "pe":
            rl += self.pe_pr
            wl += self.pe_pw
            self.pe_pr = []
            self.pe_pw = []
        for b in rl:
            b.r[sn] = v
        for b in wl:
            b.w = {sn: v}
            b.r = {}
        return inst

    def dma(self, en, out, in_, sb, reads=(), writes=()):
        e = self.eng[en]
        for b in list(writes) + list(reads):
            assert all(b is not p for p in self.pe_pw), f"dma touches {b.name} while PE write pending"
        for b in writes:
            assert all(b is not p for p in self.pe_pr), f"dma write to {b.name} while PE read pending"
        saved = None
        if sb.dsem is not None and any(b is sb for b in writes) and sb.dsem in sb.w:
            saved = sb.w.pop(sb.dsem)
        self._deps(e, reads, writes)
        if saved is not None:
            sb.w[sb.dsem] = saved
        inst = e["h"].dma_start(out=out, in_=in_)
        if sb.dsem is None:
            sb.dsem = "d_" + sb.name
            self.sems[sb.dsem] = self.es.enter_context(self.nc.semaphore(sb.dsem))
        sb.dcnt += 16
        self.dtotal[sb.dsem] = sb.dcnt
        inst.then_inc(self.sems[sb.dsem], 16)
        tok = {sb.dsem: sb.dcnt}
        for b in reads:
            if not b.dram:
                b.r.update(tok)
        for b in writes:
            if b.dram:
                b.w.update(tok)
            else:
                b.w = dict(tok)
                b.r = {}
        return inst

    def wait_all(self, en, bufs):
        e = self.eng[en]
        deps = {}
        for b in bufs:
            for sn, v in list(b.w.items()) + list(b.r.items()):
                if deps.get(sn, 0) < v:
                    deps[sn] = v
        self._emit_waits(e, deps)


def _bf(a):
    return np.ascontiguousarray(a.astype(np.float32).astype(ml_dtypes.bfloat16))


def make_consts(seqs):
    c = {}
    smax = max(seqs)
    p = np.arange(128)
    inv_freq = 10000.0 ** (-(np.arange(0, 64, 2, dtype=np.float32)) / 64.0)
    fr = np.arange(smax, dtype=np.float32)[None, :] * inv_freq[p % 32][:, None].astype(np.float32)
    c["rope_cos"] = np.cos(fr).astype(np.float32)
    sgn = np.where((p % 64) < 32, -1.0, 1.0)[:, None]
    c["rope_sin"] = (np.sin(fr) * sgn).astype(np.float32)
    k = np.arange(128)
    ang = 2 * np.pi * np.outer(k, k) / 128.0
    c["w1c"] = _bf(np.cos(ang))
    c["w1ms"] = _bf(-np.sin(ang))
    c["w1mc"] = _bf(-np.cos(ang))
    a64 = 2 * np.pi * np.outer(np.arange(64), np.arange(64)) / 64.0
    z = np.zeros((64, 64))
    c["c64bd"] = np.block([[np.cos(a64), z], [z, np.cos(a64)]]).astype(np.float32)
    c["s64bd"] = np.block([[np.sin(a64), z], [z, np.sin(a64)]]).astype(np.float32)
    for S in sorted(set(seqs)):
        n2 = S // 128
        k1 = np.arange(128)[:, None]
        s2 = np.arange(n2)[None, :]
        tw = 2 * np.pi * (k1 * s2) / float(S)
        c[f"twc{S}"] = np.cos(tw).astype(np.float32)
        c[f"tws{S}"] = np.sin(tw).astype(np.float32)
        c[f"twms{S}"] = (-np.sin(tw)).astype(np.float32)
        a2 = 2 * np.pi * np.outer(np.arange(n2), np.arange(n2)) / float(n2)
        sc = 1.0 / math.sqrt(64.0 * S)
        w2 = np.zeros((128, n2), np.float32)
        w2[0:n2] = np.cos(a2) * sc
        w2[n2:2 * n2] = np.sin(a2) * sc
        c[f"w2_{S}"] = _bf(w2)
    j = np.arange(128)[:, None]
    cc = np.arange(256)[None, :]
    c["mask"] = _bf(((j <= cc) & (cc <= j + 128)).astype(np.float32))
    pp = np.arange(64)[:, None]
    c1 = np.arange(128)[None, :]
    mf = np.zeros((128, 128), np.float32)
    mf[0:64] = (c1 <= pp + 64)
    c["maskf"] = _bf(mf)
    jj = np.arange(128)[:, None]
    c2 = np.arange(128)[None, :]
    c["maskb"] = _bf((np.abs(jj - c2) <= 64).astype(np.float32))
    c["ones"] = _bf(np.ones((128, 128)))
    return c


class Builder:
    def __init__(self, seqs, consts, phases=(1, 2, 3, 4, 5, 6), debug=False):
        self.seqs = list(seqs)
        self.TOK = sum(seqs)
        self.NT = self.TOK // T
        self.consts = consts
        self.phases = phases
        self.debug = debug
        self.nc = bass.Bass("TRN2", target_bir_lowering=False)
        self.dram = {}

    def din(self, name, shape, dt=F32):
        t = self.nc.dram_tensor(name, list(shape), dt, kind="ExternalInput").ap()
        self.dram[name] = t
        return t

    def dscr(self, name, shape, dt, out=False):
        kind = "ExternalOutput" if (out or self.debug) else "Internal"
        t = self.nc.dram_tensor(name, list(shape), dt, kind=kind).ap()
        self.dram[name] = t
        return t

    def sb(self, st, name, shape, dt):
        return st.enter_context(self.nc.sbuf_tensor(name, list(shape), dt))

    def build(self):
        nc = self.nc
        TOK = self.TOK
        i_ = self.din
        self.xT = i_("xT", [D, TOK])
        self.w = {}
        for pre in ("f1", "f2"):
            self.w[pre + "g"] = i_(pre + "_wg", [D, DFF])
            self.w[pre + "u"] = i_(pre + "_wu", [D, DFF])
            self.w[pre + "d"] = i_(pre + "_wd", [DFF, D])
        self.w_in = i_("w_in", [D, 2560])
        self.w_out = i_("w_out", [D, D])
        self.fw_pad = i_("fw_pad", [2, 128, 256])
        self.gains = i_("gains", [128, 40])
        cin = {}
        for k, v in self.consts.items():
            cin[k] = i_("c_" + k, v.shape, BF16 if v.dtype == ml_dtypes.bfloat16 else F32)
        self.cin = cin
        self.x1T = self.dscr("x1T", [D, TOK], F32)
        self.qT = self.dscr("qT", [768, TOK], BF16)
        self.kT = self.dscr("kT", [768, TOK], BF16)
        self.vaug = self.dscr("vaug", [6, TOK, 256], BF16)
        self.ab = self.dscr("ab", [TOK, 512], BF16)
        self.zs = self.dscr("zs", [len(self.seqs), 128, 64, 512], BF16)
        self.fT = self.dscr("fT", [256, TOK], BF16)
        self.oT = self.dscr("oT", [768, TOK], F32)
        self.dens = self.dscr("dens", [12, TOK], F32)
        self.rdens = self.dscr("rdens", [12, TOK], F32)
        self.x2T = self.dscr("x2T", [D, TOK], F32)
        self.yT = self.dscr("yT", [D, TOK], F32, out=True)

        with contextlib.ExitStack() as es:
            self.S = S = Sched(nc, es)
            self.B = {n: S.buf(n, dram=True) for n in
                      ("x1T", "qT", "kT", "vaug", "ab", "zs", "fT", "oT", "dens", "rdens", "x2T", "yT")}
            self.g_sb = self.sb(es, "g_sb", [128, 40], F32)
            self.ones = self.sb(es, "ones", [128, 128], BF16)
            self.epsb = self.sb(es, "epsb", [128, 1], F32)
            self.Bc = S.buf("consts")
            S.dma("sp", self.g_sb[:], self.gains[:, :], self.Bc, writes=[self.Bc])
            S.dma("sp", self.ones[:], cin["ones"][:, :], self.Bc, writes=[self.Bc])
            S.op("dve", lambda e: e.memset(self.epsb[:], EPS), writes=[self.Bc])
            self.ps = [es.enter_context(nc.psum_tensor(f"ps{i}", [128, 512], F32)) for i in range(8)]
            self.Bps = [S.buf(f"ps{i}") for i in range(8)]

            if 1 in self.phases:
                self.ffn_phase("f1", self.xT, None, self.x1T, self.B["x1T"], 0, None)
            if 2 in self.phases:
                self.proj_phase()
            if 3 in self.phases:
                self.fourier_phase()
            if 4 in self.phases:
                self.attn_phase()
            if 5 in self.phases:
                self.wout_phase()
            if 6 in self.phases:
                self.ffn_phase("f2", self.x2T, self.B["x2T"], self.yT, self.B["yT"], 24, 32)
            S.wait_all("sp", list(self.B.values()))
            S.wait_all("pool", list(self.B.values()))
        return nc

    def emit_norm(self, xt, xbuf, gc0, out_t, out_buf, sq, sqb, rstd, rstdb, pbank, nchunks=8, c0=0,
                  inv_n=1.0 / 1024, mul_eng=("dve",), oc0=None):
        S = self.S
        pn = self.ps[pbank]
        pb = self.Bps[pbank]
        W = xt.shape[-1]
        for ci in range(nchunks):
            c = c0 + ci
            s, sbf = sq[ci % 2], sqb[ci % 2]
            S.op("act", lambda e, s=s, c=c: e.activation(out=s[:, 0:W], in_=xt[:, c, :], func=AF.Square),
                 reads=[xbuf], writes=[sbf])
            S.op("pe", lambda e, s=s, ci=ci: e.matmul(pn[:, 0:W], lhsT=self.ones[:], rhs=s[:, 0:W],
                                                      start=(ci == 0), stop=(ci == nchunks - 1)),
                 reads=[sbf, self.Bc], writes=[pb], sig=True)
        S.op("act", lambda e: e.activation(out=rstd[:, 0:W], in_=pn[:, 0:W], func=AF.Sqrt,
                                           bias=self.epsb[:], scale=inv_n),
             reads=[pb, self.Bc], writes=[rstdb])
        S.op("dve", lambda e: e.reciprocal(out=rstd[:, 0:W], in_=rstd[:, 0:W]), reads=[rstdb], writes=[rstdb])
        for ci in range(nchunks):
            c = c0 + ci
            en = mul_eng[ci % len(mul_eng)]
            oc = c if oc0 is None else oc0 + ci
            S.op(en, lambda e, c=c, oc=oc, ci=ci: e.scalar_tensor_tensor(
                out=out_t[:, oc, :], in0=xt[:, c, :], scalar=self.g_sb[:, gc0 + ci:gc0 + ci + 1],
                in1=rstd[:, 0:W], op0=ALU.mult, op1=ALU.mult),
                reads=[xbuf, rstdb, self.Bc] + ([] if out_buf is xbuf else []), writes=[out_buf])

    def ffn_phase(self, pre, src, srcbuf, dst, dstbuf, gcol, fin_gcol):
        S = self.S
        nc = self.nc
        NT = self.NT
        with contextlib.ExitStack() as st:
            wg = self.sb(st, pre + "wg", [128, 8, DFF], BF16)
            wu = self.sb(st, pre + "wu", [128, 8, DFF], BF16)
            wd = self.sb(st, pre + "wd", [128, NF, D], BF16)
            xin = [self.sb(st, pre + f"xin{i}", [128, 8, T], F32) for i in range(2)]
            h = self.sb(st, pre + "h", [128, 8, T], BF16)
            sq = [self.sb(st, pre + f"sq{i}", [128, T], BF16) for i in range(2)]
            a = self.sb(st, pre + "a", [128, 11, T], BF16)
            sg = [self.sb(st, pre + f"sg{i}", [128, T], F32) for i in range(2)]
            rstd = [self.sb(st, pre + f"rstd{i}", [128, T], F32) for i in range(2)]
            PCS = ((0, 4), (4, 11), (11, 22))
            Bwg = [S.buf(f"wg{j}") for j in range(3)]
            Bwu = [S.buf(f"wu{j}") for j in range(3)]
            Bwd = [S.buf(f"wd{j}") for j in range(3)]

            def pc(F):
                return 0 if F < 4 else (1 if F < 11 else 2)
            Bx = [S.buf("xin0"), S.buf("xin1")]
            Bh, Ba = S.buf("h"), S.buf("a")
            Bsq = [S.buf("sq0"), S.buf("sq1")]
            Bsg = [S.buf("sg0"), S.buf("sg1")]
            Brs = [S.buf("rstd0"), S.buf("rstd1")]
            wgv = self.w[pre + "g"].rearrange("(k p) f -> p k f", p=128)
            wuv = self.w[pre + "u"].rearrange("(k p) f -> p k f", p=128)
            wdv = self.w[pre + "d"].rearrange("(k p) f -> p k f", p=128)
            for j, (f0, f1) in enumerate(PCS):
                for k0 in range(0, 8, 4):
                    S.dma("pool", wg[:, k0:k0 + 4, f0 * 128:f1 * 128], wgv[:, k0:k0 + 4, f0 * 128:f1 * 128], Bwg[j],
                          writes=[Bwg[j]])
                    S.dma("pool", wu[:, k0:k0 + 4, f0 * 128:f1 * 128], wuv[:, k0:k0 + 4, f0 * 128:f1 * 128], Bwu[j],
                          writes=[Bwu[j]])
                if j == 1:
                    for jj in (0, 1):
                        a0, a1 = PCS[jj]
                        S.dma("pool", wd[:, a0:a1, :], wdv[:, a0:a1, :], Bwd[jj], writes=[Bwd[jj]])
            S.dma("pool", wd[:, 11:17, :], wdv[:, 11:17, :], Bwd[2], writes=[Bwd[2]])
            S.dma("pool", wd[:, 17:22, :], wdv[:, 17:22, :], Bwd[2], writes=[Bwd[2]])
            srcv = src.rearrange("(c p) t -> p c t", p=128)
            dstv = dst.rearrange("(c p) t -> p c t", p=128)

            def load(i):
                S.dma("sp", xin[i % 2][:], srcv[:, :, i * T:(i + 1) * T], Bx[i % 2],
                      reads=([srcbuf] if srcbuf is not None else []), writes=[Bx[i % 2]])

            sq8 = self.sb(st, pre + "sq8", [128, 8, T], BF16)
            Bsq8 = [S.buf(f"sq8{c}") for c in range(8)]

            def norm_sq(xt, xb):
                for c in range(8):
                    S.op("act", lambda e, c=c, xt=xt: e.activation(out=sq8[:, c, :], in_=xt[:, c, :], func=AF.Square),
                         reads=[xb], writes=[Bsq8[c]])

            def norm_rest(xt, xb, gc0, out_t, out_b, rs, rsb, bank, with_mults=True):
                pn, pb = self.ps[bank], self.Bps[bank]
                for c in range(8):
                    S.op("pe", lambda e, c=c, pn=pn: e.matmul(pn[:], lhsT=self.ones[:], rhs=sq8[:, c, :],
                                                             start=(c == 0), stop=(c == 7)),
                         reads=[Bsq8[c], self.Bc], writes=[pb], sig=(c == 7))
                S.op("act", lambda e, pn=pn, rs=rs: e.activation(out=rs[:], in_=pn[:], func=AF.Sqrt,
                                                                 bias=self.epsb[:], scale=1.0 / 1024),
                     reads=[pb, self.Bc], writes=[rsb])
                S.op("dve", lambda e, rs=rs: e.reciprocal(out=rs[:], in_=rs[:]), reads=[rsb], writes=[rsb])
                if not with_mults:
                    return
                for c in range(8):
                    norm_mult(xt, xb, gc0, out_t, out_b, rs, rsb, c)

            def norm_mult(xt, xb, gc0, out_t, out_b, rs, rsb, c):
                S.op("dve", lambda e, c=c, xt=xt, out_t=out_t, rs=rs: e.scalar_tensor_tensor(
                    out=out_t[:, c, :], in0=xt[:, c, :], scalar=self.g_sb[:, gc0 + c:gc0 + c + 1],
                    in1=rs[:], op0=ALU.mult, op1=ALU.mult),
                    reads=[xb, rsb, self.Bc], writes=[out_b])

            def pre_sq(i):
                norm_sq(xin[i % 2], Bx[i % 2])

            def pre_rest(i):
                norm_rest(xin[i % 2], Bx[i % 2], gcol, h, Bh, rstd[0], Brs[0], 6)

            def pre_head(i):
                norm_rest(xin[i % 2], Bx[i % 2], gcol, h, Bh, rstd[0], Brs[0], 6, with_mults=False)

            def pre_mult(i, c):
                norm_mult(xin[i % 2], Bx[i % 2], gcol, h, Bh, rstd[0], Brs[0], c)

            def post_sq(i):
                norm_sq(xin[i % 2], Bx[i % 2])

            def post_rest(i):
                norm_rest(xin[i % 2], Bx[i % 2], fin_gcol, xin[i % 2], Bx[i % 2], rstd[1], Brs[1], 7)

            def post_head(i):
                norm_rest(xin[i % 2], Bx[i % 2], fin_gcol, xin[i % 2], Bx[i % 2], rstd[1], Brs[1], 7,
                          with_mults=False)

            def post_mult(i, c):
                norm_mult(xin[i % 2], Bx[i % 2], fin_gcol, xin[i % 2], Bx[i % 2], rstd[1], Brs[1], c)

            def store(i):
                S.dma("sp", dstv[:, :, i * T:(i + 1) * T], xin[i % 2][:], Bx[i % 2],
                      reads=[Bx[i % 2]], writes=[dstbuf])

            fin = fin_gcol is not None

            def hook(i, F):
                if fin and i > 0:
                    if F == 0:
                        post_sq(i - 1)
                    if F == 2:
                        post_head(i - 1)
                    if 3 <= F <= 10:
                        post_mult(i - 1, F - 3)
                    if F == 10:
                        store(i - 1)
                if F == 10 and i + 1 < NT:
                    load(i + 1)
                if i + 1 < NT:
                    if F == 17:
                        pre_sq(i + 1)
                    if F == 20:
                        pre_head(i + 1)

            def gu(i, hf):
                for f in range(11):
                    F = hf * 11 + f
                    pg, pu = self.ps[F % 2], self.ps[2 + F % 2]
                    bg, bu = self.Bps[F % 2], self.Bps[2 + F % 2]
                    for k in range(8):
                        S.op("pe", lambda e, k=k, F=F, pg=pg: e.matmul(
                            pg[:], lhsT=wg[:, k, F * 128:(F + 1) * 128], rhs=h[:, k, :],
                            start=(k == 0), stop=(k == 7)), reads=[Bwg[pc(F)], Bh], writes=[bg], sig=(k == 7))
                    for k in range(8):
                        S.op("pe", lambda e, k=k, F=F, pu=pu: e.matmul(
                            pu[:], lhsT=wu[:, k, F * 128:(F + 1) * 128], rhs=h[:, k, :],
                            start=(k == 0), stop=(k == 7)), reads=[Bwu[pc(F)], Bh], writes=[bu], sig=(k == 7))
                    s_ = sg[F % 2]
                    S.op("act", lambda e, pg=pg, s_=s_: e.activation(out=s_[:], in_=pg[:], func=AF.Silu),
                         reads=[bg], writes=[Bsg[F % 2]])
                    S.op("dve", lambda e, pu=pu, s_=s_, f=f: e.tensor_tensor(
                        out=a[:, f, :], in0=pu[:], in1=s_[:], op=ALU.mult),
                        reads=[bu, Bsg[F % 2]], writes=[Ba])
                    hook(i, F)

            def down(i, hf):
                xt = xin[i % 2]
                for d in range(8):
                    py, by = self.ps[4 + d % 2], self.Bps[4 + d % 2]
                    for f in range(11):
                        S.op("pe", lambda e, f=f, d=d, py=py: e.matmul(
                            py[:], lhsT=wd[:, hf * 11 + f, d * 128:(d + 1) * 128], rhs=a[:, f, :],
                            start=(f == 0), stop=(f == 10)), reads=[Bwd[pc(hf * 11 + f)], Ba], writes=[by], sig=(f == 10))
                    S.op("dve", lambda e, d=d, py=py, xt=xt: e.scalar_tensor_tensor(
                        out=xt[:, d, :], in0=py[:], scalar=0.5, in1=xt[:, d, :], op0=ALU.mult, op1=ALU.add),
                        reads=[by, Bx[i % 2]], writes=[Bx[i % 2]])
                    if hf == 1 and i + 1 < NT:
                        pre_mult(i + 1, d)

            load(0)
            pre_sq(0)
            pre_rest(0)
            for i in range(NT):
                gu(i, 0)
                down(i, 0)
                gu(i, 1)
                down(i, 1)
                if not fin:
                    store(i)
            if fin:
                post_sq(NT - 1)
                post_rest(NT - 1)
                store(NT - 1)
            S.barrier()

    def seq_offsets(self):
        o = 0
        res = []
        for L in self.seqs:
            res.append((o, L))
            o += L
        return res

    def proj_phase(self):
        S = self.S
        NT = self.NT
        cin = self.cin
        with contextlib.ExitStack() as st:
            win = self.sb(st, "win", [128, 8, 2560], BF16)
            xin = [self.sb(st, f"p2xin{i}", [128, 8, T], F32) for i in range(2)]
            hh = [self.sb(st, f"p2h{i}", [128, 8, T], BF16) for i in range(2)]
            sq = [self.sb(st, f"p2sq{i}", [128, T], BF16) for i in range(2)]
            rstd = self.sb(st, "p2rstd", [128, T], F32)
            cs = [self.sb(st, f"p2cs{i}", [128, T], F32) for i in range(2)]
            sn = [self.sb(st, f"p2sn{i}", [128, T], F32) for i in range(2)]
            t1d = [self.sb(st, f"p2t1d{i}", [128, T], F32) for i in range(2)]
            t2d = [self.sb(st, f"p2t2d{i}", [128, T], F32) for i in range(2)]
            nat = [self.sb(st, f"p2nat{i}", [128, T], F32) for i in range(2)]
            sw = [self.sb(st, f"p2sw{i}", [128, T], F32) for i in range(2)]
            t1a = [self.sb(st, f"p2t1a{i}", [128, T], F32) for i in range(2)]
            t2a = [self.sb(st, f"p2t2a{i}", [128, T], F32) for i in range(2)]
            qk = [self.sb(st, f"p2qk{i}", [128, 12, T], BF16) for i in range(2)]
            vo = [self.sb(st, f"p2vo{i}", [128, 4, 6, 256], BF16) for i in range(2)]
            uT = self.sb(st, "p2uT", [128, 2, T], BF16)
            abo = [self.sb(st, f"p2abo{i}", [128, 4, 512], BF16) for i in range(2)]
            mcs = self.sb(st, "p2mcs", [128, 2, 512], BF16)
            c64 = self.sb(st, "p2c64", [128, 128], F32)
            s64 = self.sb(st, "p2s64", [128, 128], F32)
            fwp = self.sb(st, "p2fwp", [128, 2, 256], F32)
            Bwin = S.buf("win")
            Bx = [S.buf("x0"), S.buf("x1")]
            Bhh = [S.buf("h0"), S.buf("h1")]
            Brs, BuT, Bmcs, Bfc = S.buf("rstd"), S.buf("uT"), S.buf("mcs"), S.buf("fc")
            Bsq = [S.buf("sq0"), S.buf("sq1")]
            Brope = [S.buf("rope0"), S.buf("rope1")]
            Bt1d = [S.buf("t1d0"), S.buf("t1d1")]
            Bt2d = [[S.buf(f"t2d{i}{q}") for q in range(4)] for i in range(2)]
            Bnat = [S.buf("nat0"), S.buf("nat1")]
            Bsw = [[S.buf(f"sw{i}{q}") for q in range(4)] for i in range(2)]
            Bt1a = [S.buf("t1a0"), S.buf("t1a1")]
            Bt2a = [S.buf("t2a0"), S.buf("t2a1")]
            QUADS = ((0, 32), (32, 0), (64, 96), (96, 64))
            Bqk = [S.buf("qk0"), S.buf("qk1")]
            Bvo = [S.buf("vo0"), S.buf("vo1")]
            Babo = [S.buf("abo0"), S.buf("abo1")]
            winv = self.w_in.rearrange("(k p) f -> p k f", p=128)
            for k in range(8):
                S.dma("pool", win[:, k, :], winv[:, k, :], Bwin, writes=[Bwin])
            S.dma("sp", c64[:], cin["c64bd"][:, :], Bfc, writes=[Bfc])
            S.dma("sp", s64[:], cin["s64bd"][:, :], Bfc, writes=[Bfc])
            S.dma("sp", fwp[:], self.fw_pad.rearrange("c p e -> p c e"), Bfc, writes=[Bfc])
            for i in range(2):
                S.op("pool", lambda e, i=i: e.memset(vo[i][:], 1.0), writes=[Bvo[i]])
            for c in range(2):
                for j, m in enumerate((c64, s64)):
                    S.op("pe", lambda e, c=c, m=m: e.matmul(self.ps[0][:, 0:256], lhsT=m[:], rhs=fwp[:, c, :],
                                                            start=True, stop=True),
                         reads=[Bfc], writes=[self.Bps[0]])
                    S.op("act", lambda e, c=c, j=j: e.activation(out=mcs[:, c, j * 256:(j + 1) * 256],
                                                                 in_=self.ps[0][:, 0:256], func=AF.Copy),
                         reads=[self.Bps[0]], writes=[Bmcs])
            srcv = self.x1T.rearrange("(c p) t -> p c t", p=128)
            qTv = self.qT.rearrange("(c p) t -> p c t", p=128)
            kTv = self.kT.rearrange("(c p) t -> p c t", p=128)
            offs = self.seq_offsets()

            def pos0(i):
                t0 = i * T
                for (o, L) in offs:
                    if o <= t0 < o + L:
                        return t0 - o
                raise AssertionError

            def load_x(i):
                b = i % 2
                S.dma("sp", xin[b][:], srcv[:, :, i * T:(i + 1) * T], Bx[b], reads=[self.B["x1T"]], writes=[Bx[b]])

            def load_rope(i):
                b = i % 2
                p0 = pos0(i)
                S.dma("sp", cs[b][:], cin["rope_cos"][:, p0:p0 + T], Brope[b], writes=[Brope[b]])
                S.dma("sp", sn[b][:], cin["rope_sin"][:, p0:p0 + T], Brope[b], writes=[Brope[b]])

            sq8 = self.sb(st, "p2sq8", [128, 8, T], BF16)
            Bsq8 = [S.buf(f"sq8{c}") for c in range(8)]

            def norm_sq_one(bx, c):
                S.op("act", lambda e, c=c, bx=bx: e.activation(out=sq8[:, c, :], in_=xin[bx][:, c, :],
                                                               func=AF.Square),
                     reads=[Bx[bx]], writes=[Bsq8[c]])

            def norm_sq(bx):
                for c in range(8):
                    norm_sq_one(bx, c)

            def norm_rest(bx):
                pn, pb = self.ps[6], self.Bps[6]
                for c in range(8):
                    S.op("pe", lambda e, c=c: e.matmul(pn[:], lhsT=self.ones[:], rhs=sq8[:, c, :],
                                                       start=(c == 0), stop=(c == 7)),
                         reads=[Bsq8[c], self.Bc], writes=[pb], sig=(c == 7))
                S.op("act", lambda e: e.activation(out=rstd[:], in_=pn[:], func=AF.Sqrt, bias=self.epsb[:],
                                                   scale=1.0 / 1024), reads=[pb, self.Bc], writes=[Brs])
                S.op("dve", lambda e: e.reciprocal(out=rstd[:], in_=rstd[:]), reads=[Brs], writes=[Brs])
                for c in range(8):
                    S.op("dve", lambda e, c=c, bx=bx: e.scalar_tensor_tensor(
                        out=hh[bx][:, c, :], in0=xin[bx][:, c, :], scalar=self.g_sb[:, 8 + c:8 + c + 1],
                        in1=rstd[:], op0=ALU.mult, op1=ALU.mult),
                        reads=[Bx[bx], Brs, self.Bc], writes=[Bhh[bx]])

            load_x(0)
            load_rope(0)
            if NT > 1:
                load_x(1)
            norm_sq(0)
            norm_rest(0)
            for i in range(NT):
                b = i % 2
                h, Bh = hh[b], Bhh[b]
                if i + 2 < NT:
                    load_x(i + 2)
                if i + 1 < NT:
                    load_rope(i + 1)
                for c in range(12):
                    pa, ba = self.ps[c % 4], self.Bps[c % 4]
                    for k in range(8):
                        S.op("pe", lambda e, k=k, c=c, pa=pa: e.matmul(
                            pa[:], lhsT=win[:, k, 256 + c * 128:256 + (c + 1) * 128], rhs=h[:, k, :],
                            start=(k == 0), stop=(k == 7)), reads=[Bwin, Bh], writes=[ba], sig=(k == 7))
                    if c <= 7 and i + 1 < NT:
                        norm_sq_one(1 - b, c)
                    j = (c // 2) % 2
                    if c % 2 == 0:
                        S.op("dve", lambda e, pa=pa, j=j: e.tensor_tensor(out=t1d[j][:], in0=pa[:], in1=cs[b][:],
                                                                          op=ALU.mult),
                             reads=[ba, Brope[b]], writes=[Bt1d[j]])
                        for q, (d0, s0) in enumerate(QUADS):
                            S.op("dve", lambda e, pa=pa, j=j, d0=d0, s0=s0: e.tensor_tensor(
                                out=t2d[j][d0:d0 + 32, :], in0=pa[s0:s0 + 32, :], in1=sn[b][d0:d0 + 32, :],
                                op=ALU.mult), reads=[ba, Brope[b]], writes=[Bt2d[j][q]])
                        S.op("pool", lambda e, c=c, j=j: e.tensor_tensor(out=qk[b][:, c, :], in0=t1d[j][:],
                                                                         in1=t2d[j][:], op=ALU.add),
                             reads=[Bt1d[j]] + Bt2d[j], writes=[Bqk[b]])
                    else:
                        S.op("act", lambda e, pa=pa, j=j: e.activation(out=nat[j][:], in_=pa[:], func=AF.Copy),
                             reads=[ba], writes=[Bnat[j]])
                        for q, (d0, s0) in enumerate(QUADS):
                            S.op("act", lambda e, pa=pa, j=j, d0=d0, s0=s0: e.activation(
                                out=sw[j][d0:d0 + 32, :], in_=pa[s0:s0 + 32, :], func=AF.Copy),
                                reads=[ba], writes=[Bsw[j][q]])
                        S.op("pool", lambda e, j=j: e.tensor_tensor(out=t1a[j][:], in0=nat[j][:], in1=cs[b][:],
                                                                    op=ALU.mult),
                             reads=[Bnat[j], Brope[b]], writes=[Bt1a[j]])
                        S.op("pool", lambda e, j=j: e.tensor_tensor(out=t2a[j][:], in0=sw[j][:], in1=sn[b][:],
                                                                    op=ALU.mult),
                             reads=Bsw[j] + [Brope[b]], writes=[Bt2a[j]])
                        S.op("pool", lambda e, c=c, j=j: e.tensor_tensor(out=qk[b][:, c, :], in0=t1a[j][:],
                                                                         in1=t2a[j][:], op=ALU.add),
                             reads=[Bt1a[j], Bt2a[j]], writes=[Bqk[b]])
                if i + 1 < NT:
                    norm_rest(1 - b)
                for c in range(2):
                    ub = 2 * c
                    for k in range(8):
                        S.op("pe", lambda e, k=k, c=c, ub=ub: e.matmul(
                            self.ps[ub][:], lhsT=win[:, k, c * 128:(c + 1) * 128], rhs=h[:, k, :],
                            start=(k == 0), stop=(k == 7)), reads=[Bwin, Bh], writes=[self.Bps[ub]],
                            sig=(k == 7))
                    S.op("act", lambda e, c=c, ub=ub: e.activation(out=uT[:, c, :], in_=self.ps[ub][:], func=AF.Copy),
                         reads=[self.Bps[ub]], writes=[BuT])
                for tt in range(4):
                    for n, (c0, ncol, hp0, nhp) in enumerate(((0, 512, 0, 4), (512, 256, 4, 2))):
                        bi = (4, 5, 7)[(tt * 2 + n) % 3]
                        pv, bv = self.ps[bi], self.Bps[bi]
                        for k in range(8):
                            S.op("pe", lambda e, k=k, tt=tt, c0=c0, ncol=ncol, pv=pv: e.matmul(
                                pv[:, 0:ncol], lhsT=h[:, k, tt * 128:(tt + 1) * 128],
                                rhs=win[:, k, 1792 + c0:1792 + c0 + ncol], start=(k == 0), stop=(k == 7)),
                                reads=[Bwin, Bh], writes=[bv], sig=(k == 7))
                        pvv = pv[:, 0:ncol].rearrange("p (a two d) -> p a two d", two=2, d=64)
                        S.op("act", lambda e, tt=tt, hp0=hp0, nhp=nhp, pvv=pvv: e.activation(
                            out=vo[b][:, tt, hp0:hp0 + nhp, 0:64], in_=pvv[:, :, 0, :], func=AF.Copy),
                            reads=[bv], writes=[Bvo[b]])
                        S.op("act", lambda e, tt=tt, hp0=hp0, nhp=nhp, pvv=pvv: e.activation(
                            out=vo[b][:, tt, hp0:hp0 + nhp, 192:256], in_=pvv[:, :, 1, :], func=AF.Copy),
                            reads=[bv], writes=[Bvo[b]])
                for tt in range(4):
                    pab, bab = self.ps[4 + tt % 2], self.Bps[4 + tt % 2]
                    for c in range(2):
                        S.op("pe", lambda e, c=c, tt=tt, pab=pab: e.matmul(
                            pab[:], lhsT=uT[:, c, tt * 128:(tt + 1) * 128], rhs=mcs[:, c, :],
                            start=(c == 0), stop=(c == 1)), reads=[BuT, Bmcs], writes=[bab], sig=(c == 1))
                    S.op("dve", lambda e, tt=tt, pab=pab: e.tensor_copy(out=abo[b][:, tt, :], in_=pab[:]),
                         reads=[bab], writes=[Babo[b]])
                sl = slice(i * T, (i + 1) * T)
                S.dma("sp", qTv[:, :, sl], qk[b][:, 0:6, :], Bqk[b], reads=[Bqk[b]], writes=[self.B["qT"]])
                S.dma("sp", kTv[:, :, sl], qk[b][:, 6:12, :], Bqk[b], reads=[Bqk[b]], writes=[self.B["kT"]])
                for tt in range(4):
                    S.dma("sp", self.vaug[:, i * T + tt * 128:i * T + (tt + 1) * 128, :].rearrange(
                        "hp p e -> p hp e"), vo[b][:, tt, :, :], Bvo[b], reads=[Bvo[b]], writes=[self.B["vaug"]])
                S.dma("sp", self.ab[sl, :].rearrange("(tt p) e -> p tt e", p=128), abo[b][:], Babo[b],
                      reads=[Babo[b]], writes=[self.B["ab"]])
            S.barrier()

    def fourier_phase(self):
        S = self.S
        cin = self.cin
        with contextlib.ExitStack() as st:
            smax = max(self.seqs)
            ain = [self.sb(st, f"p3ain{i}", [128, 16, 512], BF16) for i in range(2)]
            zo = [self.sb(st, f"p3zo{i}", [128, 16, 512], BF16) for i in range(2)]
            zT = [self.sb(st, f"p3zT{i}", [128, 32, 256], BF16) for i in range(2)]
            fsb = self.sb(st, "p3fsb", [128, 2, smax], BF16)
            m1 = [self.sb(st, f"p3m{i}", [128, 256], F32) for i in range(4)]
            w1c = self.sb(st, "p3w1c", [128, 128], BF16)
            w1ms = self.sb(st, "p3w1ms", [128, 128], BF16)
            w1mc = self.sb(st, "p3w1mc", [128, 128], BF16)
            Bk = S.buf("p3c")
            S.dma("sp", w1c[:], cin["w1c"][:, :], Bk, writes=[Bk])
            S.dma("sp", w1ms[:], cin["w1ms"][:, :], Bk, writes=[Bk])
            S.dma("sp", w1mc[:], cin["w1mc"][:, :], Bk, writes=[Bk])
            tw = {}
            for L in sorted(set(self.seqs)):
                n2 = L // 128
                d_ = {}
                for nm in ("twc", "tws", "twms"):
                    d_[nm] = self.sb(st, f"p3{nm}{L}", [128, n2], F32)
                    S.dma("sp", d_[nm][:], cin[f"{nm}{L}"][:, :], Bk, writes=[Bk])
                d_["w2"] = self.sb(st, f"p3w2{L}", [128, n2], BF16)
                S.dma("sp", d_["w2"][:], cin[f"w2_{L}"][:, :], Bk, writes=[Bk])
                tw[L] = d_
            Bain = [S.buf("ain0"), S.buf("ain1")]
            Bzo = [S.buf("zo0"), S.buf("zo1")]
            BzT = [S.buf("zT0"), S.buf("zT1")]
            Bf = S.buf("fsb")
            Bm = [S.buf(f"m{i}") for i in range(4)]
            fTv = self.fT.rearrange("(c p) t -> p c t", p=128)
            nch_tot = 0
            nz = 0
            mi = 0
            gi_tot = 0
            for si, (o, L) in enumerate(self.seq_offsets()):
                n2 = L // 128
                d_ = tw[L]
                abv = self.ab[o:o + L, :].rearrange("(s1 s2) e -> s1 s2 e", s2=n2)
                for ch in range(n2 // 16):
                    bb = nch_tot % 2
                    nch_tot += 1
                    S.dma("sp", ain[bb][:], abv[:, 16 * ch:16 * ch + 16, :], Bain[bb], reads=[self.B["ab"]],
                          writes=[Bain[bb]])
                    for sp_ in range(8):
                        s2l = 2 * sp_
                        pre, pim = self.ps[(sp_ % 2) * 2], self.ps[(sp_ % 2) * 2 + 1]
                        bre, bim = self.Bps[(sp_ % 2) * 2], self.Bps[(sp_ % 2) * 2 + 1]
                        for j in range(2):
                            A_ = ain[bb][:, s2l + j, 0:256]
                            B_ = ain[bb][:, s2l + j, 256:512]
                            prv = pre[:, j * 256:(j + 1) * 256]
                            piv = pim[:, j * 256:(j + 1) * 256]
                            S.op("pe", lambda e, prv=prv, A_=A_: e.matmul(prv, lhsT=w1c[:], rhs=A_, start=True, stop=False),
                                 reads=[Bain[bb], Bk], writes=[bre], sig=False)
                            S.op("pe", lambda e, prv=prv, B_=B_: e.matmul(prv, lhsT=w1ms[:], rhs=B_, start=False, stop=True),
                                 reads=[Bain[bb], Bk], writes=[bre], sig=True)
                            S.op("pe", lambda e, piv=piv, A_=A_: e.matmul(piv, lhsT=w1ms[:], rhs=A_, start=True, stop=False),
                                 reads=[Bain[bb], Bk], writes=[bim], sig=False)
                            S.op("pe", lambda e, piv=piv, B_=B_: e.matmul(piv, lhsT=w1mc[:], rhs=B_, start=False, stop=True),
                                 reads=[Bain[bb], Bk], writes=[bim], sig=True)
                        for j in range(2):
                            s2 = 16 * ch + s2l + j
                            ma, mb = m1[mi % 4], m1[(mi + 1) % 4]
                            Bma, Bmb = Bm[mi % 4], Bm[(mi + 1) % 4]
                            mi += 2
                            yre = pre[:, j * 256:(j + 1) * 256]
                            yim = pim[:, j * 256:(j + 1) * 256]
                            S.op("act", lambda e, ma=ma, yim=yim, s2=s2: e.activation(
                                out=ma[:], in_=yim, func=AF.Copy, scale=d_["tws"][:, s2:s2 + 1]),
                                reads=[bim, Bk], writes=[Bma])
                            S.op("act", lambda e, mb=mb, yim=yim, s2=s2: e.activation(
                                out=mb[:], in_=yim, func=AF.Copy, scale=d_["twc"][:, s2:s2 + 1]),
                                reads=[bim, Bk], writes=[Bmb])
                            S.op("dve", lambda e, ma=ma, yre=yre, s2=s2, j=j: e.scalar_tensor_tensor(
                                out=zo[bb][:, s2l + j, 0:256], in0=yre, scalar=d_["twc"][:, s2:s2 + 1], in1=ma[:],
                                op0=ALU.mult, op1=ALU.add), reads=[bre, Bma, Bk], writes=[Bzo[bb]])
                            S.op("dve", lambda e, mb=mb, yre=yre, s2=s2, j=j: e.scalar_tensor_tensor(
                                out=zo[bb][:, s2l + j, 256:512], in0=yre, scalar=d_["twms"][:, s2:s2 + 1], in1=mb[:],
                                op0=ALU.mult, op1=ALU.add), reads=[bre, Bmb, Bk], writes=[Bzo[bb]])
                    S.dma("sp", self.zs[si, :, 16 * ch:16 * ch + 16, :], zo[bb][:], Bzo[bb], reads=[Bzo[bb]],
                          writes=[self.B["zs"]])
                G = 512 // n2
                fv = fsb[:, :, 0:L].rearrange("p c (k2 k1) -> p c k2 k1", k1=128)
                for kc in range(4):
                    zb = nz % 2
                    nz += 1
                    for half in range(2):
                        S.dma("sp", zT[zb][half * n2:(half + 1) * n2, :, :],
                              self.zs[si, 32 * kc:32 * kc + 32, 0:n2, half * 256:(half + 1) * 256].rearrange(
                                  "k s e -> s k e"), BzT[zb], reads=[self.B["zs"]], writes=[BzT[zb]])
                    for ec in range(2):
                        for g0 in range(0, 32, G):
                            bi = 4 + gi_tot % 2
                            gi_tot += 1
                            pz, bz = self.ps[bi], self.Bps[bi]
                            for gg in range(G):
                                k1l = g0 + gg
                                S.op("pe", lambda e, pz=pz, gg=gg, k1l=k1l, ec=ec, zb=zb: e.matmul(
                                    pz[:, gg * n2:(gg + 1) * n2], lhsT=zT[zb][0:2 * n2, k1l, ec * 128:(ec + 1) * 128],
                                    rhs=d_["w2"][0:2 * n2, 0:n2], start=True, stop=True),
                                    reads=[BzT[zb], Bk], writes=[bz], sig=(gg == G - 1))
                            k10 = 32 * kc + g0
                            S.op("dve", lambda e, pz=pz, ec=ec, k10=k10: e.tensor_copy(
                                out=fv[:, ec, :, k10:k10 + G],
                                in_=pz[:, 0:G * n2].rearrange("p (g k) -> p k g", k=n2)),
                                reads=[bz], writes=[Bf])
                S.dma("sp", fTv[:, :, o:o + L], fsb[:, :, 0:L], Bf, reads=[Bf], writes=[self.B["fT"]])
            S.barrier()

    def attn_phase(self):
        S = self.S
        cin = self.cin
        with contextlib.ExitStack() as st:
            smax = max(self.seqs)
            nvb = max(d * ((L // d) // 128 + 1) for L in self.seqs for d in (1, 4, 16))
            qs = self.sb(st, "p4qs", [128, smax], BF16)
            ks = self.sb(st, "p4ks", [128, smax], BF16)
            acc = [self.sb(st, f"p4acc{i}", [128, smax], F32) for i in range(2)]
            vb = [self.sb(st, f"p4vb{i}", [128, nvb, 256], BF16) for i in range(2)]
            pt = [self.sb(st, f"p4pt{i}", [128, 512], BF16) for i in range(4)]
            mask = self.sb(st, "p4mask", [128, 512], BF16)
            maskf = self.sb(st, "p4maskf", [128, 128], BF16)
            maskb = self.sb(st, "p4maskb", [128, 512], BF16)
            Bk = S.buf("p4c")
            for i4 in range(4):
                S.dma("sp", maskb[:, i4 * 128:(i4 + 1) * 128], cin["maskb"][:, :], Bk, writes=[Bk])
            S.dma("sp", mask[:, 0:256], cin["mask"][:, :], Bk, writes=[Bk])
            S.dma("sp", mask[:, 256:512], cin["mask"][:, :], Bk, writes=[Bk])
            S.dma("sp", maskf[:], cin["maskf"][:, :], Bk, writes=[Bk])
            Bq, Bkk = S.buf("qs"), S.buf("ks")
            Bacc = [S.buf("accA"), S.buf("accB")]
            Bvb = [S.buf("vb0"), S.buf("vb1")]
            Bpt = [S.buf(f"pt{i}") for i in range(4)]
            npat = 0
            nun = [0, 0]
            ngrp = [0, 0]

            def load_v(o, L, hp, d, vbuf, Bv):
                Lm = L // d
                nqt = Lm // 128
                if nqt == 1:
                    S.dma("sp", vbuf[:, 0:d, :], self.vaug[hp, o:o + L, :].rearrange("(m r) e -> m r e", r=d), Bv,
                          reads=[self.B["vaug"]], writes=[Bv])
                    return
                for r in range(d):
                    base = r * (nqt + 1)
                    vr = self.vaug[hp, o:o + L, :].rearrange("(m dd) e -> dd m e", dd=d)[r]
                    S.dma("sp", vbuf[0:64, base, :], vr[0:64, :], Bv, reads=[self.B["vaug"]], writes=[Bv])
                    if nqt > 1:
                        S.dma("sp", vbuf[:, base + 1:base + nqt, :],
                              vr[64:64 + 128 * (nqt - 1), :].rearrange("(kb j) e -> j kb e", j=128), Bv,
                              reads=[self.B["vaug"]], writes=[Bv])
                    S.dma("sp", vbuf[0:64, base + nqt, :], vr[Lm - 64:Lm, :], Bv, reads=[self.B["vaug"]],
                          writes=[Bv])

            def geom(kb, nqt, Lm):
                if kb == 0:
                    return 64, 128, 0, 0, maskf[0:64, 0:128]
                if kb == nqt:
                    return 64, 128, Lm - 64, Lm - 128, mask[0:64, 0:128]
                return 128, 256, 128 * kb - 64, 128 * (kb - 1), mask[:, 0:256]

            def stage1q(rd):
                d, rs = rd["d"], rd["rs"]
                rd["par"] = []
                for X in range(2):
                    hb = 64 * X
                    par = nun[X] % 2
                    nun[X] += 1
                    rd["par"].append(par)
                    pst, bst = self.ps[2 * X + par], self.Bps[2 * X + par]
                    p_, Bp = pt[2 * X + par], Bpt[2 * X + par]
                    n = len(rs)
                    for j, r in enumerate(rs):
                        ksl = ks[hb:hb + 64, r:r + 127 * d + 1:d]
                        qsl = qs[hb:hb + 64, r:r + 127 * d + 1:d]
                        S.op("pe", lambda e, pst=pst, ksl=ksl, qsl=qsl, j=j: e.matmul(
                            pst[:, j * 128:(j + 1) * 128], lhsT=ksl, rhs=qsl, start=True, stop=True),
                            reads=[Bq, Bkk], writes=[bst], sig=(j == n - 1))
                    S.op("act", lambda e, pst=pst, p_=p_, n=n: e.activation(
                        out=p_[:, 0:128 * n], in_=pst[:, 0:128 * n], func=AF.Exp, scale=0.125),
                        reads=[bst], writes=[Bp])
                    S.op("dve", lambda e, p_=p_, n=n: e.tensor_tensor(
                        out=p_[:, 0:128 * n], in0=p_[:, 0:128 * n], in1=maskb[:, 0:128 * n], op=ALU.mult),
                        reads=[Bp, Bk], writes=[Bp])

            def stage2q(rd):
                d, rs, L = rd["d"], rd["rs"], rd["L"]
                vbuf, Bv = rd["vb"]
                n = len(rs)
                for X in range(2):
                    par = rd["par"][X]
                    p_, Bp = pt[2 * X + par], Bpt[2 * X + par]
                    gpar = rd["g0"][X] % 2
                    pa_, ba_ = self.ps[4 + 2 * X + gpar], self.Bps[4 + 2 * X + gpar]
                    for j, r in enumerate(rs):
                        S.op("pe", lambda e, pa_=pa_, j=j, r=r, X=X, p_=p_: e.matmul(
                            pa_[:, j * 128:(j + 1) * 128], lhsT=vbuf[:, r, 128 * X:128 * X + 128],
                            rhs=p_[:, j * 128:(j + 1) * 128], start=True, stop=True),
                            reads=[Bv, Bp], writes=[ba_], sig=(j == n - 1))
                    av = acc[X][:, 0:L].rearrange("p (m r) -> p r m", r=d)[:, rs[0]:rs[0] + n, :]
                    pv_ = pa_[:, 0:128 * n].rearrange("p (r m) -> p r m", m=128)
                    S.op("dve", lambda e, av=av, pv_=pv_: e.tensor_tensor(out=av, in0=pv_, in1=av, op=ALU.add),
                         reads=[ba_, Bacc[X]], writes=[Bacc[X]])

            def stage1(rd):
                if "rs" in rd:
                    return stage1q(rd)
                d, r, kbs, nqt, Lm = rd["d"], rd["r"], rd["kbs"], rd["nqt"], rd["Lm"]
                rd["par"] = []
                for X in range(2):
                    hb = 64 * X
                    par = nun[X] % 2
                    nun[X] += 1
                    rd["par"].append(par)
                    pst, bst = self.ps[2 * X + par], self.Bps[2 * X + par]
                    p_, Bp = pt[2 * X + par], Bpt[2 * X + par]
                    for j, kb in enumerate(kbs):
                        kp, ncol, k0, q0, mk = geom(kb, nqt, Lm)
                        ksl = ks[hb:hb + 64, k0 * d + r:k0 * d + r + (kp - 1) * d + 1:d]
                        qsl = qs[hb:hb + 64, q0 * d + r:q0 * d + r + (ncol - 1) * d + 1:d]
                        S.op("pe", lambda e, pst=pst, ksl=ksl, qsl=qsl, kp=kp, ncol=ncol, j=j: e.matmul(
                            pst[0:kp, j * 256:j * 256 + ncol], lhsT=ksl, rhs=qsl, start=True, stop=True),
                            reads=[Bq, Bkk], writes=[bst], sig=(j == len(kbs) - 1))
                    if len(kbs) == 2:
                        S.op("act", lambda e, pst=pst, p_=p_: e.activation(
                            out=p_[:, :], in_=pst[:, :], func=AF.Exp, scale=0.125), reads=[bst], writes=[Bp])
                        S.op("pool" if (X == 1 and nun[X] % 2 == 0) else "dve", lambda e, p_=p_: e.tensor_tensor(
                            out=p_[:, :], in0=p_[:, :], in1=mask[:, :], op=ALU.mult), reads=[Bp, Bk], writes=[Bp])
                    else:
                        kp, ncol, k0, q0, mk = geom(kbs[0], nqt, Lm)
                        S.op("act", lambda e, pst=pst, p_=p_, kp=kp, ncol=ncol: e.activation(
                            out=p_[0:kp, 0:ncol], in_=pst[0:kp, 0:ncol], func=AF.Exp, scale=0.125),
                            reads=[bst], writes=[Bp])
                        S.op("dve", lambda e, p_=p_, kp=kp, ncol=ncol, mk=mk: e.tensor_tensor(
                            out=p_[0:kp, 0:ncol], in0=p_[0:kp, 0:ncol], in1=mk, op=ALU.mult),
                            reads=[Bp, Bk], writes=[Bp])

            def stage2(rd):
                if "rs" in rd:
                    return stage2q(rd)
                d, r, kbs, nqt, Lm, G = rd["d"], rd["r"], rd["kbs"], rd["nqt"], rd["Lm"], rd["G"]
                vbuf, Bv = rd["vb"]
                for X in range(2):
                    par = rd["par"][X]
                    p_, Bp = pt[2 * X + par], Bpt[2 * X + par]
                    for j, kb in enumerate(kbs):
                        kp, ncol, k0, q0, mk = geom(kb, nqt, Lm)
                        lhs = vbuf[0:kp, r * (nqt + 1) + kb, 128 * X:128 * X + 128]
                        contribs = []
                        if kb >= 1:
                            contribs.append((kb - 1, p_[0:kp, j * 256:j * 256 + 128], False))
                        if kb <= nqt - 1:
                            c0 = 0 if kb == 0 else 128
                            contribs.append((kb, p_[0:kp, j * 256 + c0:j * 256 + c0 + 128], True))
                        for (qt, rhs, first) in contribs:
                            g = qt // G
                            gpar = (rd["g0"][X] + g) % 2
                            pa_, ba_ = self.ps[4 + 2 * X + gpar], self.Bps[4 + 2 * X + gpar]
                            col = (qt % G) * 128
                            S.op("pe", lambda e, pa_=pa_, col=col, lhs=lhs, rhs=rhs, first=first: e.matmul(
                                pa_[:, col:col + 128], lhsT=lhs, rhs=rhs, start=first, stop=(not first)),
                                reads=[Bv, Bp], writes=[ba_], sig=True)
                            if (not first) and (qt % G == G - 1):
                                base = (128 * G * g) * d + r
                                av = acc[X][:, base:base + (128 * G - 1) * d + 1:d]
                                if d == 1:
                                    S.op("act", lambda e, av=av, pa_=pa_, G=G: e.activation(
                                        out=av, in_=pa_[:, 0:128 * G], func=AF.Copy), reads=[ba_], writes=[Bacc[X]])
                                else:
                                    S.op("dve", lambda e, av=av, pa_=pa_, G=G: e.tensor_tensor(
                                        out=av, in0=pa_[:, 0:128 * G], in1=av, op=ALU.add),
                                        reads=[ba_, Bacc[X]], writes=[Bacc[X]])

            items = [(o, L, hp) for (o, L) in self.seq_offsets() for hp in range(6)]
            pats = [(o, L, hp, d) for (o, L, hp) in items for d in (1, 4, 16)]
            vslot = {}
            for i, pp in enumerate(pats):
                vslot[pp] = i % 2
            load_v(*pats[0], vb[0], Bvb[0])
            pending = None
            pi = 0
            for (o, L, hp) in items:
                S.dma("sp", qs[:, 0:L], self.qT[hp * 128:(hp + 1) * 128, o:o + L], Bq, reads=[self.B["qT"]],
                      writes=[Bq])
                S.dma("sp", ks[:, 0:L], self.kT[hp * 128:(hp + 1) * 128, o:o + L], Bkk, reads=[self.B["kT"]],
                      writes=[Bkk])
                for d in (1, 4, 16):
                    Lm = L // d
                    nqt = Lm // 128
                    G = min(4, nqt)
                    sl = vslot[(o, L, hp, d)]
                    prefetched = False
                    if nqt == 1:
                        assert d > 1
                        for r0 in range(0, d, 4):
                            rd = dict(d=d, rs=list(range(r0, r0 + 4)), L=L, vb=(vb[sl], Bvb[sl]), g0=list(ngrp))
                            stage1(rd)
                            if pending is not None:
                                stage2(pending)
                            pending = rd
                            if not prefetched and pi + 1 < len(pats):
                                nsl = vslot[pats[pi + 1]]
                                load_v(*pats[pi + 1], vb[nsl], Bvb[nsl])
                                prefetched = True
                            for X in range(2):
                                ngrp[X] += 1
                        pi += 1
                        continue
                    for r in range(d):
                        kbl = [[0]]
                        inner = list(range(1, nqt))
                        while inner:
                            kbl.append(inner[:2])
                            inner = inner[2:]
                        kbl.append([nqt])
                        for kbs in kbl:
                            rd = dict(d=d, r=r, kbs=kbs, nqt=nqt, Lm=Lm, G=G, vb=(vb[sl], Bvb[sl]),
                                      g0=list(ngrp))
                            stage1(rd)
                            if pending is not None:
                                stage2(pending)
                            pending = rd
                            if not prefetched and pi + 1 < len(pats):
                                nsl = vslot[pats[pi + 1]]
                                load_v(*pats[pi + 1], vb[nsl], Bvb[nsl])
                                prefetched = True
                        for X in range(2):
                            ngrp[X] += nqt // G
                    pi += 1
                if pending is not None:
                    stage2(pending)
                    pending = None
                S.dma("sp", self.oT[hp * 128:hp * 128 + 64, o:o + L], acc[0][0:64, 0:L], Bacc[0],
                      reads=[Bacc[0]], writes=[self.B["oT"]])
                S.dma("sp", self.dens[2 * hp:2 * hp + 1, o:o + L], acc[0][64:65, 0:L], Bacc[0],
                      reads=[Bacc[0]], writes=[self.B["dens"]])
                S.dma("sp", self.oT[hp * 128 + 64:hp * 128 + 128, o:o + L], acc[1][64:128, 0:L], Bacc[1],
                      reads=[Bacc[1]], writes=[self.B["oT"]])
                S.dma("sp", self.dens[2 * hp + 1:2 * hp + 2, o:o + L], acc[1][0:1, 0:L], Bacc[1],
                      reads=[Bacc[1]], writes=[self.B["dens"]])
            S.barrier()

    def wout_phase(self):
        S = self.S
        NT = self.NT
        with contextlib.ExitStack() as st:
            wo = self.sb(st, "p5wo", [128, 8, D], BF16)
            xin = [self.sb(st, f"p5xin{i}", [128, 8, T], F32) for i in range(3)]
            mf = [self.sb(st, f"p5mf{i}", [128, 2, T], BF16) for i in range(3)]
            ma = [self.sb(st, f"p5ma{i}", [128, 6, T], F32) for i in range(3)]
            mix = self.sb(st, "p5mix", [128, 8, T], BF16)
            sq = [self.sb(st, f"p5sq{i}", [128, T], BF16) for i in range(2)]
            rstd = [self.sb(st, f"p5rstd{i}", [128, T], F32) for i in range(2)]
            Bwo = S.buf("wo")
            Bx = [S.buf(f"x{i}") for i in range(3)]
            Bmf = [S.buf(f"mf{i}") for i in range(3)]
            Bma = [S.buf(f"ma{i}") for i in range(3)]
            Bmix = S.buf("mix")
            Bsq = [S.buf("sq0"), S.buf("sq1")]
            Brs = [S.buf("rs0"), S.buf("rs1")]
            wov = self.w_out.rearrange("(k p) f -> p k f", p=128)
            for k in range(0, 8, 2):
                S.dma("pool", wo[:, k:k + 2, :], wov[:, k:k + 2, :], Bwo, writes=[Bwo])
            nd = self.TOK // 8
            dn = self.sb(st, "p5dn", [96, nd], F32)
            Bdn = S.buf("dn")
            S.dma("sp", dn[:], self.dens.rearrange("h (a f) -> (h a) f", f=nd), Bdn, reads=[self.B["dens"]],
                  writes=[Bdn])
            S.op("dve", lambda e: e.reciprocal(out=dn[:], in_=dn[:]), reads=[Bdn], writes=[Bdn])
            S.dma("sp", self.rdens.rearrange("h (a f) -> (h a) f", f=nd), dn[:], Bdn, reads=[Bdn],
                  writes=[self.B["rdens"]])
            rd_ = [self.sb(st, f"p5rd{i}", [128, 6, T], F32) for i in range(3)]
            Brd = [S.buf(f"rd{i}") for i in range(3)]
            x1v = self.x1T.rearrange("(c p) t -> p c t", p=128)
            fv = self.fT.rearrange("(c p) t -> p c t", p=128)
            ov = self.oT.rearrange("(c p) t -> p c t", p=128)
            x2v = self.x2T.rearrange("(c p) t -> p c t", p=128)

            mixs = [mix, self.sb(st, "p5mix1", [128, 8, T], BF16)]
            Bmixs = [Bmix, S.buf("mix1")]

            def load(i):
                b = i % 3
                sl = slice(i * T, (i + 1) * T)
                S.dma("sp", xin[b][:], x1v[:, :, sl], Bx[b], reads=[self.B["x1T"]], writes=[Bx[b]])
                S.dma("sp", mf[b][:], fv[:, :, sl], Bmf[b], reads=[self.B["fT"]], writes=[Bmf[b]])
                S.dma("sp", ma[b][:], ov[:, :, sl], Bma[b], reads=[self.B["oT"]], writes=[Bma[b]])
                for c in range(6):
                    for X in range(2):
                        S.dma("sp", rd_[b][64 * X:64 * X + 64, c, :],
                              self.rdens[2 * c + X:2 * c + X + 1, sl].partition_broadcast(64), Brd[b],
                              reads=[self.B["rdens"]], writes=[Brd[b]])

            def prep(i):
                b = i % 3
                mb = i % 2
                S.op("pool", lambda e, b=b: e.tensor_tensor(out=ma[b][:], in0=ma[b][:], in1=rd_[b][:], op=ALU.mult),
                     reads=[Bma[b], Brd[b]], writes=[Bma[b]])
                self.emit_norm(mf[b], Bmf[b], 16, mixs[mb], Bmixs[mb], sq, Bsq, rstd[0], Brs[0], 6, nchunks=2,
                               inv_n=1.0 / 256, oc0=0)
                self.emit_norm(ma[b], Bma[b], 18, mixs[mb], Bmixs[mb], sq, Bsq, rstd[1], Brs[1], 7, nchunks=6,
                               inv_n=1.0 / 768, oc0=2)

            def mm(i):
                b = i % 3
                mb = i % 2
                for dch in range(8):
                    py, by = self.ps[dch % 4], self.Bps[dch % 4]
                    for k in range(8):
                        S.op("pe", lambda e, k=k, dch=dch, py=py: e.matmul(
                            py[:], lhsT=wo[:, k, dch * 128:(dch + 1) * 128], rhs=mixs[mb][:, k, :],
                            start=(k == 0), stop=(k == 7)), reads=[Bwo, Bmixs[mb]], writes=[by], sig=(k == 7))
                    S.op("dve", lambda e, dch=dch, py=py: e.tensor_tensor(
                        out=xin[b][:, dch, :], in0=py[:], in1=xin[b][:, dch, :], op=ALU.add),
                        reads=[by, Bx[b]], writes=[Bx[b]])
                S.dma("sp", x2v[:, :, i * T:(i + 1) * T], xin[b][:], Bx[b], reads=[Bx[b]], writes=[self.B["x2T"]])

            load(0)
            if NT > 1:
                load(1)
            prep(0)
            for i in range(NT):
                if i + 2 < NT:
                    load(i + 2)
                if i + 1 < NT:
                    prep(i + 1)
                mm(i)
            S.barrier()

SEQS = (8192, 2048, 2048, 2048, 2048)


def _gain_cols(*vecs):
    cols = []
    for v in vecs:
        v = np.asarray(v, np.float32).reshape(-1, 128)
        cols.append(v.T)
    return np.ascontiguousarray(np.concatenate(cols, axis=1))


def prep_shared(inp, seqs):
    consts = make_consts(seqs)
    sh = {}
    sh["f1_wg"] = np.ascontiguousarray(inp["ffn1_w_gate"][0])
    sh["f1_wu"] = np.ascontiguousarray(inp["ffn1_w_up"][0])
    sh["f1_wd"] = np.ascontiguousarray(inp["ffn1_w_down"][0])
    sh["f2_wg"] = np.ascontiguousarray(inp["ffn2_w_gate"][0])
    sh["f2_wu"] = np.ascontiguousarray(inp["ffn2_w_up"][0])
    sh["f2_wd"] = np.ascontiguousarray(inp["ffn2_w_down"][0])
    w_in = np.asarray(inp["w_in"][0])
    sh["w_in"] = np.ascontiguousarray(w_in)
    sh["w_out"] = np.ascontiguousarray(inp["w_out"][0])
    fw = np.asarray(inp["fourier_w"][0])
    fwp = np.zeros((2, 128, 256), np.float32)
    for g in range(4):
        c, gl = g // 2, g % 2
        fwp[c, gl * 64:(gl + 1) * 64, g * 64:(g + 1) * 64] = fw[g]
    sh["fw_pad"] = fwp
    sh["gains"] = _gain_cols(inp["ffn1_norm"][0], inp["mix_norm"][0], inp["fourier_out_norm"][0],
                             inp["attn_out_norm"][0], inp["ffn2_norm"][0], inp["final_norm"])
    for k, v in consts.items():
        sh["c_" + k] = v
    return consts, sh


_CACHE = {}


def kernel(**inp):
    inp = {k: np.asarray(v) for k, v in inp.items()}
    consts, sh = prep_shared(inp, SEQS)
    xp, xs = inp["x_prompt"], inp["x_sample"]
    in_maps = []
    for c in range(NCORES):
        xc = np.concatenate([xp[c], xs[4 * c:4 * c + 4].reshape(-1, D)], axis=0)
        m = dict(sh)
        m["xT"] = np.ascontiguousarray(xc.T)
        in_maps.append(m)
    b = Builder(SEQS, consts)
    nc = b.build()
    res = run_bass_kernel_spmd(nc, in_maps, core_ids=list(range(NCORES)))
    yp = np.empty_like(xp)
    ys = np.empty_like(xs)
    for c in range(NCORES):
        y = res.results[c]["yT"].T
        yp[c] = y[:8192]
        ys[4 * c:4 * c + 4] = y[8192:].reshape(4, 2048, D)
    return (yp, ys)
```

```python
import contextlib
import math

import numpy as np
import ml_dtypes

import concourse.bass as bass
import concourse.mybir as mybir
from concourse.bass_utils import run_bass_kernel_spmd

F32 = mybir.dt.float32
BF16 = mybir.dt.bfloat16
AF = mybir.ActivationFunctionType
ALU = mybir.AluOpType

D = 1024
DFF = 2816
NF = DFF // 128
T = 512
EPS = 1e-6
NCORES = 8


class Buf:
    __slots__ = ("name", "w", "r", "dsem", "dcnt", "dram")

    def __init__(self, name, dram=False):
        self.name = name
        self.w = {}
        self.r = {}
        self.dsem = None
        self.dcnt = 0
        self.dram = dram


class Sched:
    def __init__(self, nc, es):
        self.nc = nc
        self.es = es
        self.sems = {}
        self.eng = {}
        for name, h in (("pe", nc.tensor), ("act", nc.scalar), ("dve", nc.vector),
                        ("pool", nc.gpsimd), ("sp", nc.sync)):
            sn = "s_" + name
            self.sems[sn] = es.enter_context(nc.semaphore(sn))
            self.eng[name] = dict(h=h, sn=sn, cnt=0, seen={}, name=name)
        self.pe_pr = []
        self.pe_pw = []
        self.nbuf = 0
        self.dtotal = {}
        self.allbufs = []

    def buf(self, name, dram=False):
        self.nbuf += 1
        b = Buf(f"{name}_{self.nbuf}", dram)
        if not dram:
            self.allbufs.append(b)
        return b

    def barrier(self):
        for en in self.eng:
            self.wait_all(en, self.allbufs)
        self.allbufs = [b for b in self.allbufs if b.name.startswith("ps") or b.name.startswith("consts")]

    def _deps(self, e, reads, writes):
        deps = {}
        own = e["sn"]
        ispe = e["name"] == "pe"

        def add(tok, war):
            for sn, v in tok.items():
                if sn == own and (ispe or war):
                    continue
                if deps.get(sn, 0) < v:
                    deps[sn] = v
        for b in reads:
            add(b.w, False)
        for b in writes:
            if b.dram:
                continue
            add(b.w, False)
            add(b.r, True)
        self._emit_waits(e, deps)

    def _emit_waits(self, e, deps):
        for sn, v in deps.items():
            if sn in self.dtotal:
                v = max(v, self.dtotal[sn])
            if e["seen"].get(sn, 0) >= v:
                continue
            e["h"].wait_ge(self.sems[sn], v)
            e["seen"][sn] = v

    def op(self, en, fn, reads=(), writes=(), sig=True):
        e = self.eng[en]
        if en != "pe":
            for b in writes:
                assert all(b is not p for p in self.pe_pr), f"write to {b.name} while PE read pending"
                assert all(b is not p for p in self.pe_pw), f"write to {b.name} while PE write pending"
            for b in reads:
                assert all(b is not p for p in self.pe_pw), f"read of {b.name} while PE write pending"
        self._deps(e, reads, writes)
        inst = fn(e["h"])
        if en == "pe" and not sig:
            self.pe_pr.extend(reads)
            self.pe_pw.extend(writes)
            return inst
        e["cnt"] += 1
        inst.then_inc(self.sems[e["sn"]], 1)
        sn, v = e["sn"], e["cnt"]
        rl, wl = list(reads), list(writes)
        if en == "pe":
            rl += self.pe_pr
            wl += self.pe_pw
            self.pe_pr = []
            self.pe_pw = []
        for b in rl:
            b.r[sn] = v
        for b in wl:
            b.w = {sn: v}
            b.r = {}
        return inst

    def dma(self, en, out, in_, sb, reads=(), writes=()):
        e = self.eng[en]
        for b in list(writes) + list(reads):
            assert all(b is not p for p in self.pe_pw), f"dma touches {b.name} while PE write pending"
        for b in writes:
            assert all(b is not p for p in self.pe_pr), f"dma write to {b.name} while PE read pending"
        saved = None
        if sb.dsem is not None and any(b is sb for b in writes) and sb.dsem in sb.w:
            saved = sb.w.pop(sb.dsem)
        self._deps(e, reads, writes)
        if saved is not None:
            sb.w[sb.dsem] = saved
        inst = e["h"].dma_start(out=out, in_=in_)
        if sb.dsem is None:
            sb.dsem = "d_" + sb.name
            self.sems[sb.dsem] = self.es.enter_context(self.nc.semaphore(sb.dsem))
        sb.dcnt += 16
        self.dtotal[sb.dsem] = sb.dcnt
        inst.then_inc(self.sems[sb.dsem], 16)
        tok = {sb.dsem: sb.dcnt}
        for b in reads:
            if not b.dram:
                b.r.update(tok)
        for b in writes:
            if b.dram:
                b.w.update(tok)
            else:
                b.w = dict(tok)
                b.r = {}
        return inst

    def wait_all(self, en, bufs):
        e = self.eng[en]
        deps = {}
        for b in bufs:
            for sn, v in list(b.w.items()) + list(b.r.items()):
                if deps.get(sn, 0) < v:
                    deps[sn] = v
        self._emit_waits(e, deps)


def _bf(a):
    return np.ascontiguousarray(a.astype(np.float32).astype(ml_dtypes.bfloat16))


def make_consts(seqs):
    c = {}
    smax = max(seqs)
    p = np.arange(128)
    inv_freq = 10000.0 ** (-(np.arange(0, 64, 2, dtype=np.float32)) / 64.0)
    fr = np.arange(smax, dtype=np.float32)[None, :] * inv_freq[p % 32][:, None].astype(np.float32)
    c["rope_cos"] = np.cos(fr).astype(np.float32)
    sgn = np.where((p % 64) < 32, -1.0, 1.0)[:, None]
    c["rope_sin"] = (np.sin(fr) * sgn).astype(np.float32)
    k = np.arange(128)
    ang = 2 * np.pi * np.outer(k, k) / 128.0
    c["w1c"] = _bf(np.cos(ang))
    c["w1ms"] = _bf(-np.sin(ang))
    c["w1mc"] = _bf(-np.cos(ang))
    a64 = 2 * np.pi * np.outer(np.arange(64), np.arange(64)) / 64.0
    z = np.zeros((64, 64))
    c["c64bd"] = np.block([[np.cos(a64), z], [z, np.cos(a64)]]).astype(np.float32)
    c["s64bd"] = np.block([[np.sin(a64), z], [z, np.sin(a64)]]).astype(np.float32)
    for S in sorted(set(seqs)):
        n2 = S // 128
        k1 = np.arange(128)[:, None]
        s2 = np.arange(n2)[None, :]
        tw = 2 * np.pi * (k1 * s2) / float(S)
        c[f"twc{S}"] = np.cos(tw).astype(np.float32)
        c[f"tws{S}"] = np.sin(tw).astype(np.float32)
        c[f"twms{S}"] = (-np.sin(tw)).astype(np.float32)
        a2 = 2 * np.pi * np.outer(np.arange(n2), np.arange(n2)) / float(n2)
        sc = 1.0 / math.sqrt(64.0 * S)
        w2 = np.zeros((128, n2), np.float32)
        w2[0:n2] = np.cos(a2) * sc
        w2[n2:2 * n2] = np.sin(a2) * sc
        c[f"w2_{S}"] = _bf(w2)
    j = np.arange(128)[:, None]
    cc = np.arange(256)[None, :]
    c["mask"] = _bf(((j <= cc) & (cc <= j + 128)).astype(np.float32))
    pp = np.arange(64)[:, None]
    c1 = np.arange(128)[None, :]
    mf = np.zeros((128, 128), np.float32)
    mf[0:64] = (c1 <= pp + 64)
    c["maskf"] = _bf(mf)
    jj = np.arange(128)[:, None]
    c2 = np.arange(128)[None, :]
    c["maskb"] = _bf((np.abs(jj - c2) <= 64).astype(np.float32))
    c["ones"] = _bf(np.ones((128, 128)))
    return c


class Builder:
    def __init__(self, seqs, consts, phases=(1, 2, 3, 4, 5, 6), debug=False):
        self.seqs = list(seqs)
        self.TOK = sum(seqs)
        self.NT = self.TOK // T
        self.consts = consts
        self.phases = phases
        self.debug = debug
        self.nc = bass.Bass("TRN2", target_bir_lowering=False)
        self.dram = {}

    def din(self, name, shape, dt=F32):
        t = self.nc.dram_tensor(name, list(shape), dt, kind="ExternalInput").ap()
        self.dram[name] = t
        return t

    def dscr(self, name, shape, dt, out=False):
        kind = "ExternalOutput" if (out or self.debug) else "Internal"
        t = self.nc.dram_tensor(name, list(shape), dt, kind=kind).ap()
        self.dram[name] = t
        return t

    def sb(self, st, name, shape, dt):
        return st.enter_context(self.nc.sbuf_tensor(name, list(shape), dt))

    def build(self):
        nc = self.nc
        TOK = self.TOK
        i_ = self.din
        self.xT = i_("xT", [D, TOK])
        self.w = {}
        for pre in ("f1", "f2"):
            self.w[pre + "g"] = i_(pre + "_wg", [D, DFF])
            self.w[pre + "u"] = i_(pre + "_wu", [D, DFF])
            self.w[pre + "d"] = i_(pre + "_wd", [DFF, D])
        self.w_in = i_("w_in", [D, 2560])
        self.w_out = i_("w_out", [D, D])
        self.fw_pad = i_("fw_pad", [2, 128, 256])
        self.gains = i_("gains", [128, 40])
        cin = {}
        for k, v in self.consts.items():
            cin[k] = i_("c_" + k, v.shape, BF16 if v.dtype == ml_dtypes.bfloat16 else F32)
        self.cin = cin
        self.x1T = self.dscr("x1T", [D, TOK], F32)
        self.qT = self.dscr("qT", [768, TOK], BF16)
        self.kT = self.dscr("kT", [768, TOK], BF16)
        self.vaug = self.dscr("vaug", [6, TOK, 256], BF16)
        self.ab = self.dscr("ab", [TOK, 512], BF16)
        self.zs = self.dscr("zs", [len(self.seqs), 128, 64, 512], BF16)
        self.fT = self.dscr("fT", [256, TOK], BF16)
        self.oT = self.dscr("oT", [768, TOK], F32)
        self.dens = self.dscr("dens", [12, TOK], F32)
        self.rdens = self.dscr("rdens", [12, TOK], F32)
        self.x2T = self.dscr("x2T", [D, TOK], F32)
        self.yT = self.dscr("yT", [D, TOK], F32, out=True)

        with contextlib.ExitStack() as es:
            self.S = S = Sched(nc, es)
            self.B = {n: S.buf(n, dram=True) for n in
                      ("x1T", "qT", "kT", "vaug", "ab", "zs", "fT", "oT", "dens", "rdens", "x2T", "yT")}
            self.g_sb = self.sb(es, "g_sb", [128, 40], F32)
            self.ones = self.sb(es, "ones", [128, 128], BF16)
            self.epsb = self.sb(es, "epsb", [128, 1], F32)
            self.Bc = S.buf("consts")
            S.dma("sp", self.g_sb[:], self.gains[:, :], self.Bc, writes=[self.Bc])
            S.dma("sp", self.ones[:], cin["ones"][:, :], self.Bc, writes=[self.Bc])
            S.op("dve", lambda e: e.memset(self.epsb[:], EPS), writes=[self.Bc])
            self.ps = [es.enter_context(nc.psum_tensor(f"ps{i}", [128, 512], F32)) for i in range(8)]
            self.Bps = [S.buf(f"ps{i}") for i in range(8)]

            if 1 in self.phases:
                self.ffn_phase("f1", self.xT, None, self.x1T, self.B["x1T"], 0, None)
            if 2 in self.phases:
                self.proj_phase()
            if 3 in self.phases:
                self.fourier_phase()
            if 4 in self.phases:
                self.attn_phase()
            if 5 in self.phases:
                self.wout_phase()
            if 6 in self.phases:
                self.ffn_phase("f2", self.x2T, self.B["x2T"], self.yT, self.B["yT"], 24, 32)
            S.wait_all("sp", list(self.B.values()))
            S.wait_all("pool", list(self.B.values()))
        return nc

    def emit_norm(self, xt, xbuf, gc0, out_t, out_buf, sq, sqb, rstd, rstdb, pbank, nchunks=8, c0=0,
                  inv_n=1.0 / 1024, mul_eng=("dve",), oc0=None):
        S = self.S
        pn = self.ps[pbank]
        pb = self.Bps[pbank]
        W = xt.shape[-1]
        for ci in range(nchunks):
            c = c0 + ci
            s, sbf = sq[ci % 2], sqb[ci % 2]
            S.op("act", lambda e, s=s, c=c: e.activation(out=s[:, 0:W], in_=xt[:, c, :], func=AF.Square),
                 reads=[xbuf], writes=[sbf])
            S.op("pe", lambda e, s=s, ci=ci: e.matmul(pn[:, 0:W], lhsT=self.ones[:], rhs=s[:, 0:W],
                                                      start=(ci == 0), stop=(ci == nchunks - 1)),
                 reads=[sbf, self.Bc], writes=[pb], sig=True)
        S.op("act", lambda e: e.activation(out=rstd[:, 0:W], in_=pn[:, 0:W], func=AF.Sqrt,
                                           bias=self.epsb[:], scale=inv_n),
             reads=[pb, self.Bc], writes=[rstdb])
        S.op("dve", lambda e: e.reciprocal(out=rstd[:, 0:W], in_=rstd[:, 0:W]), reads=[rstdb], writes=[rstdb])
        for ci in range(nchunks):
            c = c0 + ci
            en = mul_eng[ci % len(mul_eng)]
            oc = c if oc0 is None else oc0 + ci
            S.op(en, lambda e, c=c, oc=oc, ci=ci: e.scalar_tensor_tensor(
                out=out_t[:, oc, :], in0=xt[:, c, :], scalar=self.g_sb[:, gc0 + ci:gc0 + ci + 1],
                in1=rstd[:, 0:W], op0=ALU.mult, op1=ALU.mult),
                reads=[xbuf, rstdb, self.Bc] + ([] if out_buf is xbuf else []), writes=[out_buf])

    def ffn_phase(self, pre, src, srcbuf, dst, dstbuf, gcol, fin_gcol):
        S = self.S
        nc = self.nc
        NT = self.NT
        with contextlib.ExitStack() as st:
            wg = self.sb(st, pre + "wg", [128, 8, DFF], BF16)
            wu = self.sb(st, pre + "wu", [128, 8, DFF], BF16)
            wd = self.sb(st, pre + "wd", [128, NF, D], BF16)
            xin = [self.sb(st, pre + f"xin{i}", [128, 8, T], F32) for i in range(2)]
            h = self.sb(st, pre + "h", [128, 8, T], BF16)
            sq = [self.sb(st, pre + f"sq{i}", [128, T], BF16) for i in range(2)]
            a = self.sb(st, pre + "a", [128, 11, T], BF16)
            sg = [self.sb(st, pre + f"sg{i}", [128, T], F32) for i in range(2)]
            rstd = [self.sb(st, pre + f"rstd{i}", [128, T], F32) for i in range(2)]
            PCS = ((0, 4), (4, 11), (11, 22))
            Bwg = [S.buf(f"wg{j}") for j in range(3)]
            Bwu = [S.buf(f"wu{j}") for j in range(3)]
            Bwd = [S.buf(f"wd{j}") for j in range(3)]

            def pc(F):
                return 0 if F < 4 else (1 if F < 11 else 2)
            Bx = [S.buf("xin0"), S.buf("xin1")]
            Bh, Ba = S.buf("h"), S.buf("a")
            Bsq = [S.buf("sq0"), S.buf("sq1")]
            Bsg = [S.buf("sg0"), S.buf("sg1")]
            Brs = [S.buf("rstd0"), S.buf("rstd1")]
            wgv = self.w[pre + "g"].rearrange("(k p) f -> p k f", p=128)
            wuv = self.w[pre + "u"].rearrange("(k p) f -> p k f", p=128)
            wdv = self.w[pre + "d"].rearrange("(k p) f -> p k f", p=128)
            for j, (f0, f1) in enumerate(PCS):
                for k0 in range(0, 8, 4):
                    S.dma("pool", wg[:, k0:k0 + 4, f0 * 128:f1 * 128], wgv[:, k0:k0 + 4, f0 * 128:f1 * 128], Bwg[j],
                          writes=[Bwg[j]])
                    S.dma("pool", wu[:, k0:k0 + 4, f0 * 128:f1 * 128], wuv[:, k0:k0 + 4, f0 * 128:f1 * 128], Bwu[j],
                          writes=[Bwu[j]])
                if j == 1:
                    for jj in (0, 1):
                        a0, a1 = PCS[jj]
                        S.dma("pool", wd[:, a0:a1, :], wdv[:, a0:a1, :], Bwd[jj], writes=[Bwd[jj]])
            S.dma("pool", wd[:, 11:17, :], wdv[:, 11:17, :], Bwd[2], writes=[Bwd[2]])
            S.dma("pool", wd[:, 17:22, :], wdv[:, 17:22, :], Bwd[2], writes=[Bwd[2]])
            srcv = src.rearrange("(c p) t -> p c t", p=128)
            dstv = dst.rearrange("(c p) t -> p c t", p=128)

            def load(i):
                S.dma("sp", xin[i % 2][:], srcv[:, :, i * T:(i + 1) * T], Bx[i % 2],
                      reads=([srcbuf] if srcbuf is not None else []), writes=[Bx[i % 2]])

            sq8 = self.sb(st, pre + "sq8", [128, 8, T], BF16)
            Bsq8 = [S.buf(f"sq8{c}") for c in range(8)]

            def norm_sq(xt, xb):
                for c in range(8):
                    S.op("act", lambda e, c=c, xt=xt: e.activation(out=sq8[:, c, :], in_=xt[:, c, :], func=AF.Square),
                         reads=[xb], writes=[Bsq8[c]])

            def norm_rest(xt, xb, gc0, out_t, out_b, rs, rsb, bank, with_mults=True):
                pn, pb = self.ps[bank], self.Bps[bank]
                for c in range(8):
                    S.op("pe", lambda e, c=c, pn=pn: e.matmul(pn[:], lhsT=self.ones[:], rhs=sq8[:, c, :],
                                                             start=(c == 0), stop=(c == 7)),
                         reads=[Bsq8[c], self.Bc], writes=[pb], sig=(c == 7))
                S.op("act", lambda e, pn=pn, rs=rs: e.activation(out=rs[:], in_=pn[:], func=AF.Sqrt,
                                                                 bias=self.epsb[:], scale=1.0 / 1024),
                     reads=[pb, self.Bc], writes=[rsb])
                S.op("dve", lambda e, rs=rs: e.reciprocal(out=rs[:], in_=rs[:]), reads=[rsb], writes=[rsb])
                if not with_mults:
                    return
                for c in range(8):
                    norm_mult(xt, xb, gc0, out_t, out_b, rs, rsb, c)

            def norm_mult(xt, xb, gc0, out_t, out_b, rs, rsb, c):
                S.op("dve", lambda e, c=c, xt=xt, out_t=out_t, rs=rs: e.scalar_tensor_tensor(
                    out=out_t[:, c, :], in0=xt[:, c, :], scalar=self.g_sb[:, gc0 + c:gc0 + c + 1],
                    in1=rs[:], op0=ALU.mult, op1=ALU.mult),
                    reads=[xb, rsb, self.Bc], writes=[out_b])

            def pre_sq(i):
                norm_sq(xin[i % 2], Bx[i % 2])

            def pre_rest(i):
                norm_rest(xin[i % 2], Bx[i % 2], gcol, h, Bh, rstd[0], Brs[0], 6)

            def pre_head(i):
                norm_rest(xin[i % 2], Bx[i % 2], gcol, h, Bh, rstd[0], Brs[0], 6, with_mults=False)

            def pre_mult(i, c):
                norm_mult(xin[i % 2], Bx[i % 2], gcol, h, Bh, rstd[0], Brs[0], c)

            def post_sq(i):
                norm_sq(xin[i % 2], Bx[i % 2])

            def post_rest(i):
                norm_rest(xin[i % 2], Bx[i % 2], fin_gcol, xin[i % 2], Bx[i % 2], rstd[1], Brs[1], 7)

            def post_head(i):
                norm_rest(xin[i % 2], Bx[i % 2], fin_gcol, xin[i % 2], Bx[i % 2], rstd[1], Brs[1], 7,
                          with_mults=False)

            def post_mult(i, c):
                norm_mult(xin[i % 2], Bx[i % 2], fin_gcol, xin[i % 2], Bx[i % 2], rstd[1], Brs[1], c)

            def store(i):
                S.dma("sp", dstv[:, :, i * T:(i + 1) * T], xin[i % 2][:], Bx[i % 2],
                      reads=[Bx[i % 2]], writes=[dstbuf])

            fin = fin_gcol is not None

            def hook(i, F):
                if fin and i > 0:
                    if F == 0:
                        post_sq(i - 1)
                    if F == 2:
                        post_head(i - 1)
                    if 3 <= F <= 10:
                        post_mult(i - 1, F - 3)
                    if F == 10:
                        store(i - 1)
                if F == 10 and i + 1 < NT:
                    load(i + 1)
                if i + 1 < NT:
                    if F == 17:
                        pre_sq(i + 1)
                    if F == 20:
                        pre_head(i + 1)

            def gu(i, hf):
                for f in range(11):
                    F = hf * 11 + f
                    pg, pu = self.ps[F % 2], self.ps[2 + F % 2]
                    bg, bu = self.Bps[F % 2], self.Bps[2 + F % 2]
                    for k in range(8):
                        S.op("pe", lambda e, k=k, F=F, pg=pg: e.matmul(
                            pg[:], lhsT=wg[:, k, F * 128:(F + 1) * 128], rhs=h[:, k, :],
                            start=(k == 0), stop=(k == 7)), reads=[Bwg[pc(F)], Bh], writes=[bg], sig=(k == 7))
                    for k in range(8):
                        S.op("pe", lambda e, k=k, F=F, pu=pu: e.matmul(
                            pu[:], lhsT=wu[:, k, F * 128:(F + 1) * 128], rhs=h[:, k, :],
                            start=(k == 0), stop=(k == 7)), reads=[Bwu[pc(F)], Bh], writes=[bu], sig=(k == 7))
                    s_ = sg[F % 2]
                    S.op("act", lambda e, pg=pg, s_=s_: e.activation(out=s_[:], in_=pg[:], func=AF.Silu),
                         reads=[bg], writes=[Bsg[F % 2]])
                    S.op("dve", lambda e, pu=pu, s_=s_, f=f: e.tensor_tensor(
                        out=a[:, f, :], in0=pu[:], in1=s_[:], op=ALU.mult),
                        reads=[bu, Bsg[F % 2]], writes=[Ba])
                    hook(i, F)

            def down(i, hf):
                xt = xin[i % 2]
                for d in range(8):
                    py, by = self.ps[4 + d % 2], self.Bps[4 + d % 2]
                    for f in range(11):
                        S.op("pe", lambda e, f=f, d=d, py=py: e.matmul(
                            py[:], lhsT=wd[:, hf * 11 + f, d * 128:(d + 1) * 128], rhs=a[:, f, :],
                            start=(f == 0), stop=(f == 10)), reads=[Bwd[pc(hf * 11 + f)], Ba], writes=[by], sig=(f == 10))
                    S.op("dve", lambda e, d=d, py=py, xt=xt: e.scalar_tensor_tensor(
                        out=xt[:, d, :], in0=py[:], scalar=0.5, in1=xt[:, d, :], op0=ALU.mult, op1=ALU.add),
                        reads=[by, Bx[i % 2]], writes=[Bx[i % 2]])
                    if hf == 1 and i + 1 < NT:
                        pre_mult(i + 1, d)

            load(0)
            pre_sq(0)
            pre_rest(0)
            for i in range(NT):
                gu(i, 0)
                down(i, 0)
                gu(i, 1)
                down(i, 1)
                if not fin:
                    store(i)
            if fin:
                post_sq(NT - 1)
                post_rest(NT - 1)
                store(NT - 1)
            S.barrier()

    def seq_offsets(self):
        o = 0
        res = []
        for L in self.seqs:
            res.append((o, L))
            o += L
        return res

    def proj_phase(self):
        S = self.S
        NT = self.NT
        cin = self.cin
        with contextlib.ExitStack() as st:
            win = self.sb(st, "win", [128, 8, 2560], BF16)
            xin = [self.sb(st, f"p2xin{i}", [128, 8, T], F32) for i in range(2)]
            hh = [self.sb(st, f"p2h{i}", [128, 8, T], BF16) for i in range(2)]
            sq = [self.sb(st, f"p2sq{i}", [128, T], BF16) for i in range(2)]
            rstd = self.sb(st, "p2rstd", [128, T], F32)
            cs = [self.sb(st, f"p2cs{i}", [128, T], F32) for i in range(2)]
            sn = [self.sb(st, f"p2sn{i}", [128, T], F32) for i in range(2)]
            t1d = [self.sb(st, f"p2t1d{i}", [128, T], F32) for i in range(2)]
            t2d = [self.sb(st, f"p2t2d{i}", [128, T], F32) for i in range(2)]
            nat = [self.sb(st, f"p2nat{i}", [128, T], F32) for i in range(2)]
            sw = [self.sb(st, f"p2sw{i}", [128, T], F32) for i in range(2)]
            t1a = [self.sb(st, f"p2t1a{i}", [128, T], F32) for i in range(2)]
            t2a = [self.sb(st, f"p2t2a{i}", [128, T], F32) for i in range(2)]
            qk = [self.sb(st, f"p2qk{i}", [128, 12, T], BF16) for i in range(2)]
            vo = [self.sb(st, f"p2vo{i}", [128, 4, 6, 256], BF16) for i in range(2)]
            uT = self.sb(st, "p2uT", [128, 2, T], BF16)
            abo = [self.sb(st, f"p2abo{i}", [128, 4, 512], BF16) for i in range(2)]
            mcs = self.sb(st, "p2mcs", [128, 2, 512], BF16)
            c64 = self.sb(st, "p2c64", [128, 128], F32)
            s64 = self.sb(st, "p2s64", [128, 128], F32)
            fwp = self.sb(st, "p2fwp", [128, 2, 256], F32)
            Bwin = S.buf("win")
            Bx = [S.buf("x0"), S.buf("x1")]
            Bhh = [S.buf("h0"), S.buf("h1")]
            Brs, BuT, Bmcs, Bfc = S.buf("rstd"), S.buf("uT"), S.buf("mcs"), S.buf("fc")
            Bsq = [S.buf("sq0"), S.buf("sq1")]
            Brope = [S.buf("rope0"), S.buf("rope1")]
            Bt1d = [S.buf("t1d0"), S.buf("t1d1")]
            Bt2d = [[S.buf(f"t2d{i}{q}") for q in range(4)] for i in range(2)]
            Bnat = [S.buf("nat0"), S.buf("nat1")]
            Bsw = [[S.buf(f"sw{i}{q}") for q in range(4)] for i in range(2)]
            Bt1a = [S.buf("t1a0"), S.buf("t1a1")]
            Bt2a = [S.buf("t2a0"), S.buf("t2a1")]
            QUADS = ((0, 32), (32, 0), (64, 96), (96, 64))
            Bqk = [S.buf("qk0"), S.buf("qk1")]
            Bvo = [S.buf("vo0"), S.buf("vo1")]
            Babo = [S.buf("abo0"), S.buf("abo1")]
            winv = self.w_in.rearrange("(k p) f -> p k f", p=128)
            for k in range(8):
                S.dma("pool", win[:, k, :], winv[:, k, :], Bwin, writes=[Bwin])
            S.dma("sp", c64[:], cin["c64bd"][:, :], Bfc, writes=[Bfc])
            S.dma("sp", s64[:], cin["s64bd"][:, :], Bfc, writes=[Bfc])
            S.dma("sp", fwp[:], self.fw_pad.rearrange("c p e -> p c e"), Bfc, writes=[Bfc])
            for i in range(2):
                S.op("pool", lambda e, i=i: e.memset(vo[i][:], 1.0), writes=[Bvo[i]])
            for c in range(2):
                for j, m in enumerate((c64, s64)):
                    S.op("pe", lambda e, c=c, m=m: e.matmul(self.ps[0][:, 0:256], lhsT=m[:], rhs=fwp[:, c, :],
                                                            start=True, stop=True),
                         reads=[Bfc], writes=[self.Bps[0]])
                    S.op("act", lambda e, c=c, j=j: e.activation(out=mcs[:, c, j * 256:(j + 1) * 256],
                                                                 in_=self.ps[0][:, 0:256], func=AF.Copy),
                         reads=[self.Bps[0]], writes=[Bmcs])
            srcv = self.x1T.rearrange("(c p) t -> p c t", p=128)
            qTv = self.qT.rearrange("(c p) t -> p c t", p=128)
            kTv = self.kT.rearrange("(c p) t -> p c t", p=128)
            offs = self.seq_offsets()

            def pos0(i):
                t0 = i * T
                for (o, L) in offs:
                    if o <= t0 < o + L:
                        return t0 - o
                raise AssertionError

            def load_x(i):
                b = i % 2
                S.dma("sp", xin[b][:], srcv[:, :, i * T:(i + 1) * T], Bx[b], reads=[self.B["x1T"]], writes=[Bx[b]])

            def load_rope(i):
                b = i % 2
                p0 = pos0(i)
                S.dma("sp", cs[b][:], cin["rope_cos"][:, p0:p0 + T], Brope[b], writes=[Brope[b]])
                S.dma("sp", sn[b][:], cin["rope_sin"][:, p0:p0 + T], Brope[b], writes=[Brope[b]])

            sq8 = self.sb(st, "p2sq8", [128, 8, T], BF16)
            Bsq8 = [S.buf(f"sq8{c}") for c in range(8)]

            def norm_sq_one(bx, c):
                S.op("act", lambda e, c=c, bx=bx: e.activation(out=sq8[:, c, :], in_=xin[bx][:, c, :],
                                                               func=AF.Square),
                     reads=[Bx[bx]], writes=[Bsq8[c]])

            def norm_sq(bx):
                for c in range(8):
                    norm_sq_one(bx, c)

            def norm_rest(bx):
                pn, pb = self.ps[6], self.Bps[6]
                for c in range(8):
                    S.op("pe", lambda e, c=c: e.matmul(pn[:], lhsT=self.ones[:], rhs=sq8[:, c, :],
                                                       start=(c == 0), stop=(c == 7)),
                         reads=[Bsq8[c], self.Bc], writes=[pb], sig=(c == 7))
                S.op("act", lambda e: e.activation(out=rstd[:], in_=pn[:], func=AF.Sqrt, bias=self.epsb[:],
                                                   scale=1.0 / 1024), reads=[pb, self.Bc], writes=[Brs])
                S.op("dve", lambda e: e.reciprocal(out=rstd[:], in_=rstd[:]), reads=[Brs], writes=[Brs])
                for c in range(8):
                    S.op("dve", lambda e, c=c, bx=bx: e.scalar_tensor_tensor(
                        out=hh[bx][:, c, :], in0=xin[bx][:, c, :], scalar=self.g_sb[:, 8 + c:8 + c + 1],
                        in1=rstd[:], op0=ALU.mult, op1=ALU.mult),
                        reads=[Bx[bx], Brs, self.Bc], writes=[Bhh[bx]])

            load_x(0)
            load_rope(0)
            if NT > 1:
                load_x(1)
            norm_sq(0)
            norm_rest(0)
            for i in range(NT):
                b = i % 2
                h, Bh = hh[b], Bhh[b]
                if i + 2 < NT:
                    load_x(i + 2)
                if i + 1 < NT:
                    load_rope(i + 1)
                for c in range(12):
                    pa, ba = self.ps[c % 4], self.Bps[c % 4]
                    for k in range(8):
                        S.op("pe", lambda e, k=k, c=c, pa=pa: e.matmul(
                            pa[:], lhsT=win[:, k, 256 + c * 128:256 + (c + 1) * 128], rhs=h[:, k, :],
                            start=(k == 0), stop=(k == 7)), reads=[Bwin, Bh], writes=[ba], sig=(k == 7))
                    if c <= 7 and i + 1 < NT:
                        norm_sq_one(1 - b, c)
                    j = (c // 2) % 2
                    if c % 2 == 0:
                        S.op("dve", lambda e, pa=pa, j=j: e.tensor_tensor(out=t1d[j][:], in0=pa[:], in1=cs[b][:],
                                                                          op=ALU.mult),
                             reads=[ba, Brope[b]], writes=[Bt1d[j]])
                        for q, (d0, s0) in enumerate(QUADS):
                            S.op("dve", lambda e, pa=pa, j=j, d0=d0, s0=s0: e.tensor_tensor(
                                out=t2d[j][d0:d0 + 32, :], in0=pa[s0:s0 + 32, :], in1=sn[b][d0:d0 + 32, :],
                                op=ALU.mult), reads=[ba, Brope[b]], writes=[Bt2d[j][q]])
                        S.op("pool", lambda e, c=c, j=j: e.tensor_tensor(out=qk[b][:, c, :], in0=t1d[j][:],
                                                                         in1=t2d[j][:], op=ALU.add),
                             reads=[Bt1d[j]] + Bt2d[j], writes=[Bqk[b]])
                    else:
                        S.op("act", lambda e, pa=pa, j=j: e.activation(out=nat[j][:], in_=pa[:], func=AF.Copy),
                             reads=[ba], writes=[Bnat[j]])
                        for q, (d0, s0) in enumerate(QUADS):
                            S.op("act", lambda e, pa=pa, j=j, d0=d0, s0=s0: e.activation(
                                out=sw[j][d0:d0 + 32, :], in_=pa[s0:s0 + 32, :], func=AF.Copy),
                                reads=[ba], writes=[Bsw[j][q]])
                        S.op("pool", lambda e, j=j: e.tensor_tensor(out=t1a[j][:], in0=nat[j][:], in1=cs[b][:],
                                                                    op=ALU.mult),
                             reads=[Bnat[j], Brope[b]], writes=[Bt1a[j]])
                        S.op("pool", lambda e, j=j: e.tensor_tensor(out=t2a[j][:], in0=sw[j][:], in1=sn[b][:],
                                                                    op=ALU.mult),
                             reads=Bsw[j] + [Brope[b]], writes=[Bt2a[j]])
                        S.op("pool", lambda e, c=c, j=j: e.tensor_tensor(out=qk[b][:, c, :], in0=t1a[j][:],
                                                                         in1=t2a[j][:], op=ALU.add),
                             reads=[Bt1a[j], Bt2a[j]], writes=[Bqk[b]])
                if i + 1 < NT:
                    norm_rest(1 - b)
                for c in range(2):
                    ub = 2 * c
                    for k in range(8):
                        S.op("pe", lambda e, k=k, c=c, ub=ub: e.matmul(
                            self.ps[ub][:], lhsT=win[:, k, c * 128:(c + 1) * 128], rhs=h[:, k, :],
                            start=(k == 0), stop=(k == 7)), reads=[Bwin, Bh], writes=[self.Bps[ub]],
                            sig=(k == 7))
                    S.op("act", lambda e, c=c, ub=ub: e.activation(out=uT[:, c, :], in_=self.ps[ub][:], func=AF.Copy),
                         reads=[self.Bps[ub]], writes=[BuT])
                for tt in range(4):
                    for n, (c0, ncol, hp0, nhp) in enumerate(((0, 512, 0, 4), (512, 256, 4, 2))):
                        bi = (4, 5, 7)[(tt * 2 + n) % 3]
                        pv, bv = self.ps[bi], self.Bps[bi]
                        for k in range(8):
                            S.op("pe", lambda e, k=k, tt=tt, c0=c0, ncol=ncol, pv=pv: e.matmul(
                                pv[:, 0:ncol], lhsT=h[:, k, tt * 128:(tt + 1) * 128],
                                rhs=win[:, k, 1792 + c0:1792 + c0 + ncol], start=(k == 0), stop=(k == 7)),
                                reads=[Bwin, Bh], writes=[bv], sig=(k == 7))
                        pvv = pv[:, 0:ncol].rearrange("p (a two d) -> p a two d", two=2, d=64)
                        S.op("act", lambda e, tt=tt, hp0=hp0, nhp=nhp, pvv=pvv: e.activation(
                            out=vo[b][:, tt, hp0:hp0 + nhp, 0:64], in_=pvv[:, :, 0, :], func=AF.Copy),
                            reads=[bv], writes=[Bvo[b]])
                        S.op("act", lambda e, tt=tt, hp0=hp0, nhp=nhp, pvv=pvv: e.activation(
                            out=vo[b][:, tt, hp0:hp0 + nhp, 192:256], in_=pvv[:, :, 1, :], func=AF.Copy),
                            reads=[bv], writes=[Bvo[b]])
                for tt in range(4):
                    pab, bab = self.ps[4 + tt % 2], self.Bps[4 + tt % 2]
                    for c in range(2):
                        S.op("pe", lambda e, c=c, tt=tt, pab=pab: e.matmul(
                            pab[:], lhsT=uT[:, c, tt * 128:(tt + 1) * 128], rhs=mcs[:, c, :],
                            start=(c == 0), stop=(c == 1)), reads=[BuT, Bmcs], writes=[bab], sig=(c == 1))
                    S.op("dve", lambda e, tt=tt, pab=pab: e.tensor_copy(out=abo[b][:, tt, :], in_=pab[:]),
                         reads=[bab], writes=[Babo[b]])
                sl = slice(i * T, (i + 1) * T)
                S.dma("sp", qTv[:, :, sl], qk[b][:, 0:6, :], Bqk[b], reads=[Bqk[b]], writes=[self.B["qT"]])
                S.dma("sp", kTv[:, :, sl], qk[b][:, 6:12, :], Bqk[b], reads=[Bqk[b]], writes=[self.B["kT"]])
                for tt in range(4):
                    S.dma("sp", self.vaug[:, i * T + tt * 128:i * T + (tt + 1) * 128, :].rearrange(
                        "hp p e -> p hp e"), vo[b][:, tt, :, :], Bvo[b], reads=[Bvo[b]], writes=[self.B["vaug"]])
                S.dma("sp", self.ab[sl, :].rearrange("(tt p) e -> p tt e", p=128), abo[b][:], Babo[b],
                      reads=[Babo[b]], writes=[self.B["ab"]])
            S.barrier()

    def fourier_phase(self):
        S = self.S
        cin = self.cin
        with contextlib.ExitStack() as st:
            smax = max(self.seqs)
            ain = [self.sb(st, f"p3ain{i}", [128, 16, 512], BF16) for i in range(2)]
            zo = [self.sb(st, f"p3zo{i}", [128, 16, 512], BF16) for i in range(2)]
            zT = [self.sb(st, f"p3zT{i}", [128, 32, 256], BF16) for i in range(2)]
            fsb = self.sb(st, "p3fsb", [128, 2, smax], BF16)
            m1 = [self.sb(st, f"p3m{i}", [128, 256], F32) for i in range(4)]
            w1c = self.sb(st, "p3w1c", [128, 128], BF16)
            w1ms = self.sb(st, "p3w1ms", [128, 128], BF16)
            w1mc = self.sb(st, "p3w1mc", [128, 128], BF16)
            Bk = S.buf("p3c")
            S.dma("sp", w1c[:], cin["w1c"][:, :], Bk, writes=[Bk])
            S.dma("sp", w1ms[:], cin["w1ms"][:, :], Bk, writes=[Bk])
            S.dma("sp", w1mc[:], cin["w1mc"][:, :], Bk, writes=[Bk])
            tw = {}
            for L in sorted(set(self.seqs)):
                n2 = L // 128
                d_ = {}
                for nm in ("twc", "tws", "twms"):
                    d_[nm] = self.sb(st, f"p3{nm}{L}", [128, n2], F32)
                    S.dma("sp", d_[nm][:], cin[f"{nm}{L}"][:, :], Bk, writes=[Bk])
                d_["w2"] = self.sb(st, f"p3w2{L}", [128, n2], BF16)
                S.dma("sp", d_["w2"][:], cin[f"w2_{L}"][:, :], Bk, writes=[Bk])
                tw[L] = d_
            Bain = [S.buf("ain0"), S.buf("ain1")]
            Bzo = [S.buf("zo0"), S.buf("zo1")]
            BzT = [S.buf("zT0"), S.buf("zT1")]
            Bf = S.buf("fsb")
            Bm = [S.buf(f"m{i}") for i in range(4)]
            fTv = self.fT.rearrange("(c p) t -> p c t", p=128)
            nch_tot = 0
            nz = 0
            mi = 0
            gi_tot = 0
            for si, (o, L) in enumerate(self.seq_offsets()):
                n2 = L // 128
                d_ = tw[L]
                abv = self.ab[o:o + L, :].rearrange("(s1 s2) e -> s1 s2 e", s2=n2)
                for ch in range(n2 // 16):
                    bb = nch_tot % 2
                    nch_tot += 1
                    S.dma("sp", ain[bb][:], abv[:, 16 * ch:16 * ch + 16, :], Bain[bb], reads=[self.B["ab"]],
                          writes=[Bain[bb]])
                    for sp_ in range(8):
                        s2l = 2 * sp_
                        pre, pim = self.ps[(sp_ % 2) * 2], self.ps[(sp_ % 2) * 2 + 1]
                        bre, bim = self.Bps[(sp_ % 2) * 2], self.Bps[(sp_ % 2) * 2 + 1]
                        for j in range(2):
                            A_ = ain[bb][:, s2l + j, 0:256]
                            B_ = ain[bb][:, s2l + j, 256:512]
                            prv = pre[:, j * 256:(j + 1) * 256]
                            piv = pim[:, j * 256:(j + 1) * 256]
                            S.op("pe", lambda e, prv=prv, A_=A_: e.matmul(prv, lhsT=w1c[:], rhs=A_, start=True, stop=False),
                                 reads=[Bain[bb], Bk], writes=[bre], sig=False)
                            S.op("pe", lambda e, prv=prv, B_=B_: e.matmul(prv, lhsT=w1ms[:], rhs=B_, start=False, stop=True),
                                 reads=[Bain[bb], Bk], writes=[bre], sig=True)
                            S.op("pe", lambda e, piv=piv, A_=A_: e.matmul(piv, lhsT=w1ms[:], rhs=A_, start=True, stop=False),
                                 reads=[Bain[bb], Bk], writes=[bim], sig=False)
                            S.op("pe", lambda e, piv=piv, B_=B_: e.matmul(piv, lhsT=w1mc[:], rhs=B_, start=False, stop=True),
                                 reads=[Bain[bb], Bk], writes=[bim], sig=True)
                        for j in range(2):
                            s2 = 16 * ch + s2l + j
                            ma, mb = m1[mi % 4], m1[(mi + 1) % 4]
                            Bma, Bmb = Bm[mi % 4], Bm[(mi + 1) % 4]
                            mi += 2
                            yre = pre[:, j * 256:(j + 1) * 256]
                            yim = pim[:, j * 256:(j + 1) * 256]
                            S.op("act", lambda e, ma=ma, yim=yim, s2=s2: e.activation(
                                out=ma[:], in_=yim, func=AF.Copy, scale=d_["tws"][:, s2:s2 + 1]),
                                reads=[bim, Bk], writes=[Bma])
                            S.op("act", lambda e, mb=mb, yim=yim, s2=s2: e.activation(
                                out=mb[:], in_=yim, func=AF.Copy, scale=d_["twc"][:, s2:s2 + 1]),
                                reads=[bim, Bk], writes=[Bmb])
                            S.op("dve", lambda e, ma=ma, yre=yre, s2=s2, j=j: e.scalar_tensor_tensor(
                                out=zo[bb][:, s2l + j, 0:256], in0=yre, scalar=d_["twc"][:, s2:s2 + 1], in1=ma[:],
                                op0=ALU.mult, op1=ALU.add), reads=[bre, Bma, Bk], writes=[Bzo[bb]])
                            S.op("dve", lambda e, mb=mb, yre=yre, s2=s2, j=j: e.scalar_tensor_tensor(
                                out=zo[bb][:, s2l + j, 256:512], in0=yre, scalar=d_["twms"][:, s2:s2 + 1], in1=mb[:],
                                op0=ALU.mult, op1=ALU.add), reads=[bre, Bmb, Bk], writes=[Bzo[bb]])
                    S.dma("sp", self.zs[si, :, 16 * ch:16 * ch + 16, :], zo[bb][:], Bzo[bb], reads=[Bzo[bb]],
                          writes=[self.B["zs"]])
                G = 512 // n2
                fv = fsb[:, :, 0:L].rearrange("p c (k2 k1) -> p c k2 k1", k1=128)
                for kc in range(4):
                    zb = nz % 2
                    nz += 1
                    for half in range(2):
                        S.dma("sp", zT[zb][half * n2:(half + 1) * n2, :, :],
                              self.zs[si, 32 * kc:32 * kc + 32, 0:n2, half * 256:(half + 1) * 256].rearrange(
                                  "k s e -> s k e"), BzT[zb], reads=[self.B["zs"]], writes=[BzT[zb]])
                    for ec in range(2):
                        for g0 in range(0, 32, G):
                            bi = 4 + gi_tot % 2
                            gi_tot += 1
                            pz, bz = self.ps[bi], self.Bps[bi]
                            for gg in range(G):
                                k1l = g0 + gg
                                S.op("pe", lambda e, pz=pz, gg=gg, k1l=k1l, ec=ec, zb=zb: e.matmul(
                                    pz[:, gg * n2:(gg + 1) * n2], lhsT=zT[zb][0:2 * n2, k1l, ec * 128:(ec + 1) * 128],
                                    rhs=d_["w2"][0:2 * n2, 0:n2], start=True, stop=True),
                                    reads=[BzT[zb], Bk], writes=[bz], sig=(gg == G - 1))
                            k10 = 32 * kc + g0
                            S.op("dve", lambda e, pz=pz, ec=ec, k10=k10: e.tensor_copy(
                                out=fv[:, ec, :, k10:k10 + G],
                                in_=pz[:, 0:G * n2].rearrange("p (g k) -> p k g", k=n2)),
                                reads=[bz], writes=[Bf])
                S.dma("sp", fTv[:, :, o:o + L], fsb[:, :, 0:L], Bf, reads=[Bf], writes=[self.B["fT"]])
            S.barrier()

    def attn_phase(self):
        S = self.S
        cin = self.cin
        with contextlib.ExitStack() as st:
            smax = max(self.seqs)
            nvb = max(d * ((L // d) // 128 + 1) for L in self.seqs for d in (1, 4, 16))
            qs = self.sb(st, "p4qs", [128, smax], BF16)
            ks = self.sb(st, "p4ks", [128, smax], BF16)
            acc = [self.sb(st, f"p4acc{i}", [128, smax], F32) for i in range(2)]
            vb = [self.sb(st, f"p4vb{i}", [128, nvb, 256], BF16) for i in range(2)]
            pt = [self.sb(st, f"p4pt{i}", [128, 512], BF16) for i in range(4)]
            mask = self.sb(st, "p4mask", [128, 512], BF16)
            maskf = self.sb(st, "p4maskf", [128, 128], BF16)
            maskb = self.sb(st, "p4maskb", [128, 512], BF16)
            Bk = S.buf("p4c")
            for i4 in range(4):
                S.dma("sp", maskb[:, i4 * 128:(i4 + 1) * 128], cin["maskb"][:, :], Bk, writes=[Bk])
            S.dma("sp", mask[:, 0:256], cin["mask"][:, :], Bk, writes=[Bk])
            S.dma("sp", mask[:, 256:512], cin["mask"][:, :], Bk, writes=[Bk])
            S.dma("sp", maskf[:], cin["maskf"][:, :], Bk, writes=[Bk])
            Bq, Bkk = S.buf("qs"), S.buf("ks")
            Bacc = [S.buf("accA"), S.buf("accB")]
            Bvb = [S.buf("vb0"), S.buf("vb1")]
            Bpt = [S.buf(f"pt{i}") for i in range(4)]
            npat = 0
            nun = [0, 0]
            ngrp = [0, 0]

            def load_v(o, L, hp, d, vbuf, Bv):
                Lm = L // d
                nqt = Lm // 128
                if nqt == 1:
                    S.dma("sp", vbuf[:, 0:d, :], self.vaug[hp, o:o + L, :].rearrange("(m r) e -> m r e", r=d), Bv,
                          reads=[self.B["vaug"]], writes=[Bv])
                    return
                for r in range(d):
                    base = r * (nqt + 1)
                    vr = self.vaug[hp, o:o + L, :].rearrange("(m dd) e -> dd m e", dd=d)[r]
                    S.dma("sp", vbuf[0:64, base, :], vr[0:64, :], Bv, reads=[self.B["vaug"]], writes=[Bv])
                    if nqt > 1:
                        S.dma("sp", vbuf[:, base + 1:base + nqt, :],
                              vr[64:64 + 128 * (nqt - 1), :].rearrange("(kb j) e -> j kb e", j=128), Bv,
                              reads=[self.B["vaug"]], writes=[Bv])
                    S.dma("sp", vbuf[0:64, base + nqt, :], vr[Lm - 64:Lm, :], Bv, reads=[self.B["vaug"]],
                          writes=[Bv])

            def geom(kb, nqt, Lm):
                if kb == 0:
                    return 64, 128, 0, 0, maskf[0:64, 0:128]
                if kb == nqt:
                    return 64, 128, Lm - 64, Lm - 128, mask[0:64, 0:128]
                return 128, 256, 128 * kb - 64, 128 * (kb - 1), mask[:, 0:256]

            def stage1q(rd):
                d, rs = rd["d"], rd["rs"]
                rd["par"] = []
                for X in range(2):
                    hb = 64 * X
                    par = nun[X] % 2
                    nun[X] += 1
                    rd["par"].append(par)
                    pst, bst = self.ps[2 * X + par], self.Bps[2 * X + par]
                    p_, Bp = pt[2 * X + par], Bpt[2 * X + par]
                    n = len(rs)
                    for j, r in enumerate(rs):
                        ksl = ks[hb:hb + 64, r:r + 127 * d + 1:d]
                        qsl = qs[hb:hb + 64, r:r + 127 * d + 1:d]
                        S.op("pe", lambda e, pst=pst, ksl=ksl, qsl=qsl, j=j: e.matmul(
                            pst[:, j * 128:(j + 1) * 128], lhsT=ksl, rhs=qsl, start=True, stop=True),
                            reads=[Bq, Bkk], writes=[bst], sig=(j == n - 1))
                    S.op("act", lambda e, pst=pst, p_=p_, n=n: e.activation(
                        out=p_[:, 0:128 * n], in_=pst[:, 0:128 * n], func=AF.Exp, scale=0.125),
                        reads=[bst], writes=[Bp])
                    S.op("dve", lambda e, p_=p_, n=n: e.tensor_tensor(
                        out=p_[:, 0:128 * n], in0=p_[:, 0:128 * n], in1=maskb[:, 0:128 * n], op=ALU.mult),
                        reads=[Bp, Bk], writes=[Bp])

            def stage2q(rd):
                d, rs, L = rd["d"], rd["rs"], rd["L"]
                vbuf, Bv = rd["vb"]
                n = len(rs)
                for X in range(2):
                    par = rd["par"][X]
                    p_, Bp = pt[2 * X + par], Bpt[2 * X + par]
                    gpar = rd["g0"][X] % 2
                    pa_, ba_ = self.ps[4 + 2 * X + gpar], self.Bps[4 + 2 * X + gpar]
                    for j, r in enumerate(rs):
                        S.op("pe", lambda e, pa_=pa_, j=j, r=r, X=X, p_=p_: e.matmul(
                            pa_[:, j * 128:(j + 1) * 128], lhsT=vbuf[:, r, 128 * X:128 * X + 128],
                            rhs=p_[:, j * 128:(j + 1) * 128], start=True, stop=True),
                            reads=[Bv, Bp], writes=[ba_], sig=(j == n - 1))
                    av = acc[X][:, 0:L].rearrange("p (m r) -> p r m", r=d)[:, rs[0]:rs[0] + n, :]
                    pv_ = pa_[:, 0:128 * n].rearrange("p (r m) -> p r m", m=128)
                    S.op("dve", lambda e, av=av, pv_=pv_: e.tensor_tensor(out=av, in0=pv_, in1=av, op=ALU.add),
                         reads=[ba_, Bacc[X]], writes=[Bacc[X]])

            def stage1(rd):
                if "rs" in rd:
                    return stage1q(rd)
                d, r, kbs, nqt, Lm = rd["d"], rd["r"], rd["kbs"], rd["nqt"], rd["Lm"]
                rd["par"] = []
                for X in range(2):
                    hb = 64 * X
                    par = nun[X] % 2
                    nun[X] += 1
                    rd["par"].append(par)
                    pst, bst = self.ps[2 * X + par], self.Bps[2 * X + par]
                    p_, Bp = pt[2 * X + par], Bpt[2 * X + par]
                    for j, kb in enumerate(kbs):
                        kp, ncol, k0, q0, mk = geom(kb, nqt, Lm)
                        ksl = ks[hb:hb + 64, k0 * d + r:k0 * d + r + (kp - 1) * d + 1:d]
                        qsl = qs[hb:hb + 64, q0 * d + r:q0 * d + r + (ncol - 1) * d + 1:d]
                        S.op("pe", lambda e, pst=pst, ksl=ksl, qsl=qsl, kp=kp, ncol=ncol, j=j: e.matmul(
                            pst[0:kp, j * 256:j * 256 + ncol], lhsT=ksl, rhs=qsl, start=True, stop=True),
                            reads=[Bq, Bkk], writes=[bst], sig=(j == len(kbs) - 1))
                    if len(kbs) == 2:
                        S.op("act", lambda e, pst=pst, p_=p_: e.activation(
                            out=p_[:, :], in_=pst[:, :], func=AF.Exp, scale=0.125), reads=[bst], writes=[Bp])
                        S.op("pool" if (X == 1 and nun[X] % 2 == 0) else "dve", lambda e, p_=p_: e.tensor_tensor(
                            out=p_[:, :], in0=p_[:, :], in1=mask[:, :], op=ALU.mult), reads=[Bp, Bk], writes=[Bp])
                    else:
                        kp, ncol, k0, q0, mk = geom(kbs[0], nqt, Lm)
                        S.op("act", lambda e, pst=pst, p_=p_, kp=kp, ncol=ncol: e.activation(
                            out=p_[0:kp, 0:ncol], in_=pst[0:kp, 0:ncol], func=AF.Exp, scale=0.125),
                            reads=[bst], writes=[Bp])
                        S.op("dve", lambda e, p_=p_, kp=kp, ncol=ncol, mk=mk: e.tensor_tensor(
                            out=p_[0:kp, 0:ncol], in0=p_[0:kp, 0:ncol], in1=mk, op=ALU.mult),
                            reads=[Bp, Bk], writes=[Bp])

            def stage2(rd):
                if "rs" in rd:
                    return stage2q(rd)
                d, r, kbs, nqt, Lm, G = rd["d"], rd["r"], rd["kbs"], rd["nqt"], rd["Lm"], rd["G"]
                vbuf, Bv = rd["vb"]
                for X in range(2):
                    par = rd["par"][X]
                    p_, Bp = pt[2 * X + par], Bpt[2 * X + par]
                    for j, kb in enumerate(kbs):
                        kp, ncol, k0, q0, mk = geom(kb, nqt, Lm)
                        lhs = vbuf[0:kp, r * (nqt + 1) + kb, 128 * X:128 * X + 128]
                        contribs = []
                        if kb >= 1:
                            contribs.append((kb - 1, p_[0:kp, j * 256:j * 256 + 128], False))
                        if kb <= nqt - 1:
                            c0 = 0 if kb == 0 else 128
                            contribs.append((kb, p_[0:kp, j * 256 + c0:j * 256 + c0 + 128], True))
                        for (qt, rhs, first) in contribs:
                            g = qt // G
                            gpar = (rd["g0"][X] + g) % 2
                            pa_, ba_ = self.ps[4 + 2 * X + gpar], self.Bps[4 + 2 * X + gpar]
                            col = (qt % G) * 128
                            S.op("pe", lambda e, pa_=pa_, col=col, lhs=lhs, rhs=rhs, first=first: e.matmul(
                                pa_[:, col:col + 128], lhsT=lhs, rhs=rhs, start=first, stop=(not first)),
                                reads=[Bv, Bp], writes=[ba_], sig=True)
                            if (not first) and (qt % G == G - 1):
                                base = (128 * G * g) * d + r
                                av = acc[X][:, base:base + (128 * G - 1) * d + 1:d]
                                if d == 1:
                                    S.op("act", lambda e, av=av, pa_=pa_, G=G: e.activation(
                                        out=av, in_=pa_[:, 0:128 * G], func=AF.Copy), reads=[ba_], writes=[Bacc[X]])
                                else:
                                    S.op("dve", lambda e, av=av, pa_=pa_, G=G: e.tensor_tensor(
                                        out=av, in0=pa_[:, 0:128 * G], in1=av, op=ALU.add),
                                        reads=[ba_, Bacc[X]], writes=[Bacc[X]])

            items = [(o, L, hp) for (o, L) in self.seq_offsets() for hp in range(6)]
            pats = [(o, L, hp, d) for (o, L, hp) in items for d in (1, 4, 16)]
            vslot = {}
            for i, pp in enumerate(pats):
                vslot[pp] = i % 2
            load_v(*pats[0], vb[0], Bvb[0])
            pending = None
            pi = 0
            for (o, L, hp) in items:
                S.dma("sp", qs[:, 0:L], self.qT[hp * 128:(hp + 1) * 128, o:o + L], Bq, reads=[self.B["qT"]],
                      writes=[Bq])
                S.dma("sp", ks[:, 0:L], self.kT[hp * 128:(hp + 1) * 128, o:o + L], Bkk, reads=[self.B["kT"]],
                      writes=[Bkk])
                for d in (1, 4, 16):
                    Lm = L // d
                    nqt = Lm // 128
                    G = min(4, nqt)
                    sl = vslot[(o, L, hp, d)]
                    prefetched = False
                    if nqt == 1:
                        assert d > 1
                        for r0 in range(0, d, 4):
                            rd = dict(d=d, rs=list(range(r0, r0 + 4)), L=L, vb=(vb[sl], Bvb[sl]), g0=list(ngrp))
                            stage1(rd)
                            if pending is not None:
                                stage2(pending)
                            pending = rd
                            if not prefetched and pi + 1 < len(pats):
                                nsl = vslot[pats[pi + 1]]
                                load_v(*pats[pi + 1], vb[nsl], Bvb[nsl])
                                prefetched = True
                            for X in range(2):
                                ngrp[X] += 1
                        pi += 1
                        continue
                    for r in range(d):
                        kbl = [[0]]
                        inner = list(range(1, nqt))
                        while inner:
                            kbl.append(inner[:2])
                            inner = inner[2:]
                        kbl.append([nqt])
                        for kbs in kbl:
                            rd = dict(d=d, r=r, kbs=kbs, nqt=nqt, Lm=Lm, G=G, vb=(vb[sl], Bvb[sl]),
                                      g0=list(ngrp))
                            stage1(rd)
                            if pending is not None:
                                stage2(pending)
                            pending = rd
                            if not prefetched and pi + 1 < len(pats):
                                nsl = vslot[pats[pi + 1]]
                                load_v(*pats[pi + 1], vb[nsl], Bvb[nsl])
                                prefetched = True
                        for X in range(2):
                            ngrp[X] += nqt // G
                    pi += 1
                if pending is not None:
                    stage2(pending)
                    pending = None
                S.dma("sp", self.oT[hp * 128:hp * 128 + 64, o:o + L], acc[0][0:64, 0:L], Bacc[0],
                      reads=[Bacc[0]], writes=[self.B["oT"]])
                S.dma("sp", self.dens[2 * hp:2 * hp + 1, o:o + L], acc[0][64:65, 0:L], Bacc[0],
                      reads=[Bacc[0]], writes=[self.B["dens"]])
                S.dma("sp", self.oT[hp * 128 + 64:hp * 128 + 128, o:o + L], acc[1][64:128, 0:L], Bacc[1],
                      reads=[Bacc[1]], writes=[self.B["oT"]])
                S.dma("sp", self.dens[2 * hp + 1:2 * hp + 2, o:o + L], acc[1][0:1, 0:L], Bacc[1],
                      reads=[Bacc[1]], writes=[self.B["dens"]])
            S.barrier()

    def wout_phase(self):
        S = self.S
        NT = self.NT
        with contextlib.ExitStack() as st:
            wo = self.sb(st, "p5wo", [128, 8, D], BF16)
            xin = [self.sb(st, f"p5xin{i}", [128, 8, T], F32) for i in range(3)]
            mf = [self.sb(st, f"p5mf{i}", [128, 2, T], BF16) for i in range(3)]
            ma = [self.sb(st, f"p5ma{i}", [128, 6, T], F32) for i in range(3)]
            mix = self.sb(st, "p5mix", [128, 8, T], BF16)
            sq = [self.sb(st, f"p5sq{i}", [128, T], BF16) for i in range(2)]
            rstd = [self.sb(st, f"p5rstd{i}", [128, T], F32) for i in range(2)]
            Bwo = S.buf("wo")
            Bx = [S.buf(f"x{i}") for i in range(3)]
            Bmf = [S.buf(f"mf{i}") for i in range(3)]
            Bma = [S.buf(f"ma{i}") for i in range(3)]
            Bmix = S.buf("mix")
            Bsq = [S.buf("sq0"), S.buf("sq1")]
            Brs = [S.buf("rs0"), S.buf("rs1")]
            wov = self.w_out.rearrange("(k p) f -> p k f", p=128)
            for k in range(0, 8, 2):
                S.dma("pool", wo[:, k:k + 2, :], wov[:, k:k + 2, :], Bwo, writes=[Bwo])
            nd = self.TOK // 8
            dn = self.sb(st, "p5dn", [96, nd], F32)
            Bdn = S.buf("dn")
            S.dma("sp", dn[:], self.dens.rearrange("h (a f) -> (h a) f", f=nd), Bdn, reads=[self.B["dens"]],
                  writes=[Bdn])
            S.op("dve", lambda e: e.reciprocal(out=dn[:], in_=dn[:]), reads=[Bdn], writes=[Bdn])
            S.dma("sp", self.rdens.rearrange("h (a f) -> (h a) f", f=nd), dn[:], Bdn, reads=[Bdn],
                  writes=[self.B["rdens"]])
            rd_ = [self.sb(st, f"p5rd{i}", [128, 6, T], F32) for i in range(3)]
            Brd = [S.buf(f"rd{i}") for i in range(3)]
            x1v = self.x1T.rearrange("(c p) t -> p c t", p=128)
            fv = self.fT.rearrange("(c p) t -> p c t", p=128)
            ov = self.oT.rearrange("(c p) t -> p c t", p=128)
            x2v = self.x2T.rearrange("(c p) t -> p c t", p=128)

            mixs = [mix, self.sb(st, "p5mix1", [128, 8, T], BF16)]
            Bmixs = [Bmix, S.buf("mix1")]

            sq8 = [self.sb(st, f"p5sq8{i}", [128, 8, T], BF16) for i in range(2)]
            Bsq8 = [[S.buf(f"sq8{i}{c}") for c in range(8)] for i in range(2)]

            def Lm(i):
                b = i % 3
                sl = slice(i * T, (i + 1) * T)
                S.dma("sp", mf[b][:], fv[:, :, sl], Bmf[b], reads=[self.B["fT"]], writes=[Bmf[b]])
                S.dma("sp", ma[b][:], ov[:, :, sl], Bma[b], reads=[self.B["oT"]], writes=[Bma[b]])
                for c in range(6):
                    for X in range(2):
                        S.dma("sp", rd_[b][64 * X:64 * X + 64, c, :],
                              self.rdens[2 * c + X:2 * c + X + 1, sl].partition_broadcast(64), Brd[b],
                              reads=[self.B["rdens"]], writes=[Brd[b]])

            def Lx(i):
                b = i % 2
                S.dma("sp", xin[b][:], x1v[:, :, i * T:(i + 1) * T], Bx[b], reads=[self.B["x1T"]], writes=[Bx[b]])

            def src_chunk(i, c):
                b = i % 3
                return (mf[b][:, c, :], Bmf[b]) if c < 2 else (ma[b][:, c - 2, :], Bma[b])

            def A(i):
                b = i % 3
                S.op("pool", lambda e, b=b: e.tensor_tensor(out=ma[b][:], in0=ma[b][:], in1=rd_[b][:], op=ALU.mult),
                     reads=[Bma[b], Brd[b]], writes=[Bma[b]])
                for c in range(8):
                    ap, bf = src_chunk(i, c)
                    S.op("act", lambda e, c=c, ap=ap, i=i: e.activation(out=sq8[i % 2][:, c, :], in_=ap, func=AF.Square),
                         reads=[bf], writes=[Bsq8[i % 2][c]])

            def N(i):
                for g, (c0, nck, inv_n) in enumerate(((0, 2, 1.0 / 256), (2, 6, 1.0 / 768))):
                    pn, pb = self.ps[6 + g], self.Bps[6 + g]
                    for ci in range(nck):
                        c = c0 + ci
                        S.op("pe", lambda e, c=c, ci=ci, nck=nck, pn=pn, i=i: e.matmul(
                            pn[:], lhsT=self.ones[:], rhs=sq8[i % 2][:, c, :], start=(ci == 0), stop=(ci == nck - 1)),
                            reads=[Bsq8[i % 2][c], self.Bc], writes=[pb], sig=(ci == nck - 1))
                    S.op("act", lambda e, pn=pn, g=g, inv_n=inv_n: e.activation(
                        out=rstd[g][:], in_=pn[:], func=AF.Sqrt, bias=self.epsb[:], scale=inv_n),
                        reads=[pb, self.Bc], writes=[Brs[g]])
                    S.op("dve", lambda e, g=g: e.reciprocal(out=rstd[g][:], in_=rstd[g][:]), reads=[Brs[g]],
                         writes=[Brs[g]])

            def M(i, c):
                ap, bf = src_chunk(i, c)
                g = 0 if c < 2 else 1
                S.op("dve", lambda e, c=c, ap=ap, g=g, i=i: e.scalar_tensor_tensor(
                    out=mixs[i % 2][:, c, :], in0=ap, scalar=self.g_sb[:, 16 + c:16 + c + 1], in1=rstd[g][:],
                    op0=ALU.mult, op1=ALU.mult), reads=[bf, Brs[g], self.Bc], writes=[Bmixs[i % 2]])

            def MM(i):
                b = i % 2
                for dch in range(8):
                    py, by = self.ps[dch % 4], self.Bps[dch % 4]
                    for k in range(8):
                        S.op("pe", lambda e, k=k, dch=dch, py=py: e.matmul(
                            py[:], lhsT=wo[:, k, dch * 128:(dch + 1) * 128], rhs=mixs[b][:, k, :],
                            start=(k == 0), stop=(k == 7)), reads=[Bwo, Bmixs[b]], writes=[by], sig=(k == 7))
                    S.op("dve", lambda e, dch=dch, py=py: e.tensor_tensor(
                        out=xin[b][:, dch, :], in0=py[:], in1=xin[b][:, dch, :], op=ALU.add),
                        reads=[by, Bx[b]], writes=[Bx[b]])
                    if i + 1 < NT:
                        M(i + 1, dch)
                S.dma("sp", x2v[:, :, i * T:(i + 1) * T], xin[b][:], Bx[b], reads=[Bx[b]], writes=[self.B["x2T"]])

            for j in range(min(3, NT)):
                Lm(j)
            Lx(0)
            A(0)
            if NT > 1:
                A(1)
            N(0)
            for c in range(8):
                M(0, c)
            for i in range(NT):
                if i + 3 < NT:
                    Lm(i + 3)
                if i + 1 < NT:
                    Lx(i + 1)
                if i + 2 < NT:
                    A(i + 2)
                if i + 1 < NT:
                    N(i + 1)
                MM(i)
            S.barrier()

SEQS = (8192, 2048, 2048, 2048, 2048)


def _gain_cols(*vecs):
    cols = []
    for v in vecs:
        v = np.asarray(v, np.float32).reshape(-1, 128)
        cols.append(v.T)
    return np.ascontiguousarray(np.concatenate(cols, axis=1))


def prep_shared(inp, seqs):
    consts = make_consts(seqs)
    sh = {}
    sh["f1_wg"] = np.ascontiguousarray(inp["ffn1_w_gate"][0])
    sh["f1_wu"] = np.ascontiguousarray(inp["ffn1_w_up"][0])
    sh["f1_wd"] = np.ascontiguousarray(inp["ffn1_w_down"][0])
    sh["f2_wg"] = np.ascontiguousarray(inp["ffn2_w_gate"][0])
    sh["f2_wu"] = np.ascontiguousarray(inp["ffn2_w_up"][0])
    sh["f2_wd"] = np.ascontiguousarray(inp["ffn2_w_down"][0])
    w_in = np.asarray(inp["w_in"][0])
    sh["w_in"] = np.ascontiguousarray(w_in)
    sh["w_out"] = np.ascontiguousarray(inp["w_out"][0])
    fw = np.asarray(inp["fourier_w"][0])
    fwp = np.zeros((2, 128, 256), np.float32)
    for g in range(4):
        c, gl = g // 2, g % 2
        fwp[c, gl * 64:(gl + 1) * 64, g * 64:(g + 1) * 64] = fw[g]
    sh["fw_pad"] = fwp
    sh["gains"] = _gain_cols(inp["ffn1_norm"][0], inp["mix_norm"][0], inp["fourier_out_norm"][0],
                             inp["attn_out_norm"][0], inp["ffn2_norm"][0], inp["final_norm"])
    for k, v in consts.items():
        sh["c_" + k] = v
    return consts, sh


_CACHE = {}


def kernel(**inp):
    inp = {k: np.asarray(v) for k, v in inp.items()}
    consts, sh = prep_shared(inp, SEQS)
    xp, xs = inp["x_prompt"], inp["x_sample"]
    in_maps = []
    for c in range(NCORES):
        xc = np.concatenate([xp[c], xs[4 * c:4 * c + 4].reshape(-1, D)], axis=0)
        m = dict(sh)
        m["xT"] = np.ascontiguousarray(xc.T)
        in_maps.append(m)
    b = Builder(SEQS, consts)
    nc = b.build()
    res = run_bass_kernel_spmd(nc, in_maps, core_ids=list(range(NCORES)))
    yp = np.empty_like(xp)
    ys = np.empty_like(xs)
    for c in range(NCORES):
        y = res.results[c]["yT"].T
        yp[c] = y[:8192]
        ys[4 * c:4 * c + 4] = y[8192:].reshape(4, 2048, D)
    return (yp, ys)
```
